# Optimizing a Trainium2 kernel written in Bass

```python
import jax, jax.numpy as jnp
from jax import lax
import numpy as np

D_MODEL = 1024
BATCH = 16
SEQ = 2048
DEPTH = 2
DEC_BATCH = 8
DEC_SEQ = 64
PAST_LEN = 2048

CHUNK = 64
N_A_LAYERS = DEPTH // 2
N_B_LAYERS = DEPTH - N_A_LAYERS
CONV_WIDTH = 31
CONV_STATE = CONV_WIDTH - 1
N_HEADS = 16
HEAD_DIM = D_MODEL // N_HEADS
N_KV_HEADS = 4
GROUP = N_HEADS // N_KV_HEADS
D_FF = -(-8 * D_MODEL // (3 * 256)) * 256
Q_BLOCK = 128
EPS = 1e-6

kernel_name = "yoco_conformer_conv_stick_breaking_stream_step"


def rmsnorm(x, g):
    xf = x.astype(jnp.float32)
    y = xf * lax.rsqrt(jnp.mean(xf * xf, axis=-1, keepdims=True) + EPS)
    return (y * g.astype(jnp.float32)).astype(x.dtype)


def layernorm(x, g, b):
    xf = x.astype(jnp.float32)
    mu = jnp.mean(xf, axis=-1, keepdims=True)
    xc = xf - mu
    y = xc * lax.rsqrt(jnp.mean(xc * xc, axis=-1, keepdims=True) + EPS)
    return (y * g.astype(jnp.float32) + b.astype(jnp.float32)).astype(x.dtype)


def swiglu(xn, w_gu, w_down):
    g, u = jnp.split(xn @ w_gu, 2, axis=-1)
    return (jax.nn.silu(g) * u) @ w_down


def conv_module(xn, prefix, w_in, b_in, w_dw, b_dw, ln_g, ln_b, w_out, b_out):
    a, gate = jnp.split(xn @ w_in + b_in, 2, axis=-1)
    u = a * jax.nn.sigmoid(gate)
    u_ext = jnp.concatenate([prefix.astype(u.dtype), u], axis=1)
    y = lax.conv_general_dilated(
        u_ext, w_dw[:, None, :], window_strides=(1,), padding='VALID',
        dimension_numbers=('NWC', 'WIO', 'NWC'),
        feature_group_count=D_MODEL) + b_dw
    y = jax.nn.silu(layernorm(y, ln_g, ln_b))
    return y @ w_out + b_out, u_ext[:, -CONV_STATE:]


def stick_breaking_block(q, k, v, q_pos, k_pos):
    z = jnp.einsum('bgrqd,bgsd->bgrqs', q, k,
                   preferred_element_type=jnp.float32) * (HEAD_DIM ** -0.5)
    mask = k_pos[None, :] < q_pos[:, None]
    log_beta = jax.nn.log_sigmoid(z)
    log_keep = jnp.where(mask, log_beta - z, 0.0)
    after = lax.cumsum(log_keep, axis=4, reverse=True) - log_keep
    att = jnp.where(mask, jnp.exp(log_beta + after), 0.0)
    return jnp.einsum('bgrqs,bgsd->bgrqd', att.astype(v.dtype), v)


def sb_attention(xn, k_all, v_all, q_pos, k_pos, w_q, w_o):
    bn, t, _ = xn.shape
    q = (xn @ w_q).reshape(bn, t, N_KV_HEADS, GROUP, HEAD_DIM).transpose(0, 2, 3, 1, 4)
    kt = k_all.transpose(0, 2, 1, 3)
    vt = v_all.transpose(0, 2, 1, 3)
    if t <= Q_BLOCK:
        o = stick_breaking_block(q, kt, vt, q_pos, k_pos)
    else:
        nb = t // Q_BLOCK
        qb = q.reshape(bn, N_KV_HEADS, GROUP, nb, Q_BLOCK, HEAD_DIM).transpose(3, 0, 1, 2, 4, 5)
        pb = q_pos.reshape(nb, Q_BLOCK)
        ob = lax.map(lambda a: stick_breaking_block(a[0], kt, vt, a[1], k_pos), (qb, pb))
        o = ob.transpose(1, 2, 3, 0, 4, 5).reshape(bn, N_KV_HEADS, GROUP, t, HEAD_DIM)
    o = o.transpose(0, 3, 1, 2, 4).reshape(bn, t, N_HEADS * HEAD_DIM)
    return o @ w_o


def trunk(x, conv_prefix, k_past, v_past,
          a_norm_g, conv_w_in, conv_b_in, conv_w_dw, conv_b_dw, conv_ln_g, conv_ln_b,
          conv_w_out, conv_b_out, kv_norm_g, w_kv, b_norm_g, w_q, w_o,
          ffn_norm_g, ffn_w_gu, ffn_w_down, final_norm_g):
    bn, t, _ = x.shape
    h = x
    conv_states = []
    k_new = v_new = k_all = v_all = q_pos = k_pos = None
    for l in range(DEPTH):
        if l < N_A_LAYERS:
            mix, st = conv_module(rmsnorm(h, a_norm_g[l]), conv_prefix[l],
                                  conv_w_in[l], conv_b_in[l], conv_w_dw[l], conv_b_dw[l],
                                  conv_ln_g[l], conv_ln_b[l], conv_w_out[l], conv_b_out[l])
            h = h + mix
            conv_states.append(st)
        else:
            if l == N_A_LAYERS:
                k_new, v_new = jnp.split(rmsnorm(h, kv_norm_g) @ w_kv, 2, axis=-1)
                k_new = k_new.reshape(bn, t, N_KV_HEADS, HEAD_DIM)
                v_new = v_new.reshape(bn, t, N_KV_HEADS, HEAD_DIM)
                if k_past is None:
                    past = 0
                    k_all, v_all = k_new, v_new
                else:
                    past = k_past.shape[1]
                    k_all = jnp.concatenate([k_past.astype(k_new.dtype), k_new], axis=1)
                    v_all = jnp.concatenate([v_past.astype(v_new.dtype), v_new], axis=1)
                q_pos = past + jnp.arange(t, dtype=jnp.int32)
                k_pos = jnp.arange(k_all.shape[1], dtype=jnp.int32)
            j = l - N_A_LAYERS
            h = h + sb_attention(rmsnorm(h, b_norm_g[j]), k_all, v_all, q_pos, k_pos, w_q[j], w_o[j])
        h = h + swiglu(rmsnorm(h, ffn_norm_g[l]), ffn_w_gu[l], ffn_w_down[l])
    return rmsnorm(h, final_norm_g), jnp.stack(conv_states), k_new, v_new


def setup_inputs(seed: int = 0) -> dict:
    key = jax.random.key(seed)
    ks = jax.random.split(key, 24)
    f32 = jnp.float32
    D, HD, KVD = D_MODEL, N_HEADS * HEAD_DIM, N_KV_HEADS * HEAD_DIM

    def nrm(k, shape, scale):
        return jax.random.normal(k, shape, f32) * scale

    def gain(k, shape):
        return 1.0 + 0.05 * jax.random.normal(k, shape, f32)

    return {
        "x_prompt": nrm(ks[0], (BATCH, SEQ, D), 1.0),
        "x_sample": nrm(ks[1], (DEC_BATCH, DEC_SEQ, D), 1.0),
        "state_conv": nrm(ks[2], (N_A_LAYERS, DEC_BATCH, CONV_STATE, D), 0.5),
        "cache_k": nrm(ks[3], (DEC_BATCH, PAST_LEN, N_KV_HEADS, HEAD_DIM), 1.0),
        "cache_v": nrm(ks[4], (DEC_BATCH, PAST_LEN, N_KV_HEADS, HEAD_DIM), 1.0),
        "a_norm_g": gain(ks[5], (N_A_LAYERS, D)),
        "conv_w_in": nrm(ks[6], (N_A_LAYERS, D, 2 * D), D ** -0.5),
        "conv_b_in": nrm(ks[7], (N_A_LAYERS, 2 * D), 0.02),
        "conv_w_dw": nrm(ks[8], (N_A_LAYERS, CONV_WIDTH, D), CONV_WIDTH ** -0.5),
        "conv_b_dw": nrm(ks[9], (N_A_LAYERS, D), 0.02),
        "conv_ln_g": gain(ks[10], (N_A_LAYERS, D)),
        "conv_ln_b": nrm(ks[11], (N_A_LAYERS, D), 0.02),
        "conv_w_out": nrm(ks[12], (N_A_LAYERS, D, D), D ** -0.5),
        "conv_b_out": nrm(ks[13], (N_A_LAYERS, D), 0.02),
        "kv_norm_g": gain(ks[14], (D,)),
        "w_kv": nrm(ks[15], (D, 2 * KVD), D ** -0.5),
        "b_norm_g": gain(ks[16], (N_B_LAYERS, D)),
        "w_q": nrm(ks[17], (N_B_LAYERS, D, HD), D ** -0.5),
        "w_o": nrm(ks[18], (N_B_LAYERS, HD, D), HD ** -0.5),
        "ffn_norm_g": gain(ks[19], (DEPTH, D)),
        "ffn_w_gu": nrm(ks[20], (DEPTH, D, 2 * D_FF), D ** -0.5),
        "ffn_w_down": nrm(ks[21], (DEPTH, D_FF, D), D_FF ** -0.5),
        "final_norm_g": gain(ks[22], (D,)),
    }


def reference(x_prompt, x_sample, state_conv, cache_k, cache_v,
              a_norm_g, conv_w_in, conv_b_in, conv_w_dw, conv_b_dw, conv_ln_g, conv_ln_b,
              conv_w_out, conv_b_out, kv_norm_g, w_kv, b_norm_g, w_q, w_o,
              ffn_norm_g, ffn_w_gu, ffn_w_down, final_norm_g):
    weights = (a_norm_g, conv_w_in, conv_b_in, conv_w_dw, conv_b_dw, conv_ln_g, conv_ln_b,
               conv_w_out, conv_b_out, kv_norm_g, w_kv, b_norm_g, w_q, w_o,
               ffn_norm_g, ffn_w_gu, ffn_w_down, final_norm_g)
    zero_prefix = jnp.zeros((N_A_LAYERS, x_prompt.shape[0], CONV_STATE, D_MODEL), x_prompt.dtype)
    y_prompt, conv_state_prompt, k_prompt, v_prompt = trunk(
        x_prompt, zero_prefix, None, None, *weights)
    y_sample, conv_state_sample, k_sample, v_sample = trunk(
        x_sample, state_conv, cache_k, cache_v, *weights)
    return (y_prompt, y_sample, conv_state_prompt, k_prompt, v_prompt,
            conv_state_sample, k_sample, v_sample)
```

```python
import contextlib
import numpy as np
import concourse.bass as bass
import concourse.mybir as mybir
from concourse.bass_utils import run_bass_kernel_spmd

F32 = mybir.dt.float32
BF16 = mybir.dt.bfloat16
AF = mybir.ActivationFunctionType
ALU = mybir.AluOpType
ENG = ("pe", "act", "dve", "pool", "sp")

D = 1024
DFF = 2816
NF = 22
EPS = 1e-6


class Op:
    __slots__ = ("eng", "fn", "reads", "writes", "dma", "idx", "waits", "signal", "count", "sem", "deps", "cost")

    def __init__(self, eng, fn, reads, writes, dma, final, cost=0.0):
        self.eng = eng; self.fn = fn; self.reads = tuple(reads); self.cost = cost
        self.writes = tuple(writes) + tuple(k for k in self.reads
                                            if isinstance(k, tuple) and k and k[0] == "ps" and k not in writes)
        self.dma = dma; self.waits = {}; self.count = None; self.sem = None
        self.signal = bool(final) or (dma is not None)


class Sched:
    def __init__(self, nc):
        self.nc = nc
        self.ops = []
        self.cur = self.ops

    def op(self, eng, fn, reads=(), writes=(), dma=None, final=False, cost=0.0):
        o = Op(eng, fn, reads, writes, dma, final, cost)
        self.cur.append(o)
        return o

    @contextlib.contextmanager
    def thread(self, lst):
        prev = self.cur
        self.cur = lst
        try:
            yield lst
        finally:
            self.cur = prev

    def merge(self, A, B):
        def fracs(L):
            tot = sum(o.cost for o in L) or 1.0
            acc = 0.0
            out = []
            for o in L:
                out.append(acc / tot)
                acc += o.cost
            return out
        fa, fb = fracs(A), fracs(B)
        i = j = 0
        la, lb = len(A), len(B)
        while i < la or j < lb:
            if j >= lb or (i < la and fa[i] <= fb[j]):
                self.cur.append(A[i]); i += 1
            else:
                self.cur.append(B[j]); j += 1

    def analyze(self):
        lastw = {}
        readers = {}
        for idx, o in enumerate(self.ops):
            o.idx = idx
            deps = {}
            for k in o.reads:
                w = lastw.get(k)
                if w is not None:
                    deps[w.idx] = "raw"
            for k in o.writes:
                w = lastw.get(k)
                if w is not None and w.idx not in deps:
                    deps[w.idx] = "waw"
                for r in readers.get(k, ()):
                    if r.idx not in deps and r.idx != o.idx:
                        deps[r.idx] = "war"
            for k in o.reads:
                readers.setdefault(k, []).append(o)
            for k in o.writes:
                lastw[k] = o
                readers[k] = []
            o.deps = deps

    def finalize(self):
        self.analyze()
        ops = self.ops
        need = []
        for o in ops:
            for di, kind in o.deps.items():
                d = ops[di]
                if d.dma is None and o.dma is None and d.eng == o.eng and d.eng == "pe":
                    continue
                need.append((o, d))
                d.signal = True
        cnt = {}
        for o in ops:
            if not o.signal:
                continue
            if o.dma is not None:
                key = ("dma", o.dma)
                cnt[key] = cnt.get(key, 0) + 16
            else:
                key = ("eng", o.eng)
                cnt[key] = cnt.get(key, 0) + 1
            o.sem = key
            o.count = cnt[key]
        for o, d in need:
            prev = o.waits.get(d.sem, 0)
            if d.count > prev:
                o.waits[d.sem] = d.count
        self.semkeys = sorted(cnt.keys(), key=str)
        self.final_counts = {k: v for k, v in cnt.items() if k[0] == "dma"}
        return cnt

    def emit(self):
        nc = self.nc
        cnt = self.finalize()
        streams = {e: [] for e in ENG}
        for o in self.ops:
            streams[o.eng].append(o)
        with contextlib.ExitStack() as es:
            sems = {}
            for i, k in enumerate(self.semkeys):
                sems[k] = es.enter_context(nc.semaphore(f"s{i}"))
            block = es.enter_context(nc.Block())

            def run(engname):
                def body(eng):
                    waited = {}
                    for o in streams[engname]:
                        for sk, v in o.waits.items():
                            if waited.get(sk, 0) < v:
                                eng.wait_ge(sems[sk], v)
                                waited[sk] = v
                        ins = o.fn(eng)
                        if o.signal:
                            ins.then_inc(sems[o.sem], 16 if o.dma is not None else 1)
                    if engname == "sp":
                        for sk, v in self.final_counts.items():
                            if waited.get(sk, 0) < v:
                                eng.wait_ge(sems[sk], v)
                return body

            block.tensor(run("pe"))
            block.scalar(run("act"))
            block.vector(run("dve"))
            block.gpsimd(run("pool"))
            block.sync(run("sp"))
        return cnt


V_AG = 0; V_BIN = 8; V_WDW = 24; V_BDW = 272; V_LNG = 280; V_LNB = 288; V_BOUT = 296
V_KVG = 304; V_BG = 312; V_FG = 320; V_FIN = 336; V_EPS = 344; V_ONE = 345; V_INVD = 346; V_NBG = 347; NV = 355
C_ID = 0; C_ONES = 128; C_NTRI = 256; C_NONES = 384; C_MASK = 512; C_MASK64 = 1024; NCST = 1280


def build(NP, SEQ, PAST, TS):
    nc = bass.Bass("TRN2", target_bir_lowering=False)
    TB = 512
    NBLK = SEQ // TB
    NPC = PAST // 128
    KLEN = max(SEQ, PAST + TS)
    NVC = max(SEQ // 128, NPC + 1)

    def din(name, shape, dt=F32):
        return nc.dram_tensor(name, shape, dt, kind="ExternalInput").ap()

    def dout(name, shape):
        return nc.dram_tensor(name, shape, F32, kind="ExternalOutput").ap()

    def dscr(name, shape):
        return nc.dram_tensor(name, shape, BF16, kind="Internal").ap()

    xp = din("xp", [NP, SEQ, D]); xs = din("xs", [TS, D]); sconv = din("sconv", [30, D])
    ck = din("ck", [PAST, 256]); cv = din("cv", [PAST, 256])
    w_in = din("w_in", [D, 2 * D]); w_out = din("w_out", [D, D]); w_kv = din("w_kv", [D, 512])
    w_q = din("w_q", [D, D]); w_o = din("w_o", [D, D])
    w_gu = din("w_gu", [2, D, 2 * DFF]); w_dn = din("w_dn", [2, DFF, D])
    vecs_d = din("vecs", [128, NV]); cst_d = din("cst", [128, NCST])
    yp = dout("yp", [NP, SEQ, D]); ys = dout("ys", [TS, D]); csp = dout("csp", [NP, 30, D])
    kp = dout("kp", [NP, SEQ, 256]); vp = dout("vp", [NP, SEQ, 256]); css = dout("css", [30, D])
    ks = dout("ks", [TS, 256]); vs = dout("vs", [TS, 256])
    win_bf = dscr("win_bf", [D, 2 * D]); wout_bf = dscr("wout_bf", [D, D]); wkv_bf = dscr("wkv_bf", [D, 512])
    wq_bf = dscr("wq_bf", [D, D]); wo_bf = dscr("wo_bf", [D, D])
    wgu_bf = dscr("wgu_bf", [2, D, 2 * DFF]); wdn_bf = dscr("wdn_bf", [2, DFF, D])
    wdg_bf = dscr("wdg_bf", [8, 128, 8, 512])

    S = Sched(nc)

    def ncols(ap):
        n = 1
        for d in list(ap.shape)[1:]:
            n *= int(d)
        return n

    def MM(out, lhsT, rhs, start, stop, r, w):
        S.op("pe", lambda e: e.matmul(out, lhsT=lhsT, rhs=rhs, start=start, stop=stop, skip_group_check=True), r, w,
             cost=ncols(rhs) / 2.4 + 8)

    def TR(out, in_, ident, r, w):
        S.op("pe", lambda e: e.transpose(out, in_, ident), r, w, cost=250.0)

    def ACT(out, in_, func, r, w, bias=None, scale=None):
        kw = {}
        if bias is not None:
            kw["bias"] = bias
        if scale is not None:
            kw["scale"] = scale
        S.op("act", lambda e: e.activation(out=out, in_=in_, func=func, **kw), r, w, cost=150 + ncols(out) / 1.3)

    def TT(eng, out, in0, in1, op, r, w):
        c = 120 + ncols(out) / 0.96
        S.op(eng, lambda e: e.tensor_tensor(out=out, in0=in0, in1=in1, op=op), r, w, cost=c * (2.0 if eng == "pool" else 1.0))

    def TSC(eng, out, in0, s1, s2, op0, op1, r, w):
        S.op(eng, lambda e: e.tensor_scalar(out=out, in0=in0, scalar1=s1, scalar2=s2, op0=op0, op1=op1), r, w,
             cost=120 + ncols(out) / 0.96)

    def STT(out, in0, scalar, in1, op0, op1, r, w):
        S.op("dve", lambda e: e.scalar_tensor_tensor(out=out, in0=in0, scalar=scalar, in1=in1, op0=op0, op1=op1), r, w,
             cost=120 + ncols(out) / 0.96)

    def SIGM_FROM_E(t, key):
        ACT(t, t, AF.Ln, [key, "vecs"], [key], bias=vecs[:, V_ONE:V_ONE + 1], scale=1.0)
        ACT(t, t, AF.Exp, [key], [key], scale=-1.0)

    def CP(eng, out, in_, r, w):
        c = 120 + ncols(out) / 0.96
        S.op(eng, lambda e: e.tensor_copy(out=out, in_=in_), r, w, cost=c * (2.0 if eng == "pool" else 1.0))

    def MSET(eng, ap, val, w):
        S.op(eng, lambda e: e.memset(ap, val), (), w, cost=100.0)

    def DMA(eng, out, in_, r, w, group, final=False):
        S.op(eng, lambda e: e.dma_start(out=out, in_=in_), r, w, dma=group, final=final)

    with contextlib.ExitStack() as es:
        def sb(name, shape, dt):
            return es.enter_context(nc.sbuf_tensor(name, shape, dt))

        NSLOT = 4
        slots = [sb(f"slot{i}", [128, 8, 512], BF16) for i in range(NSLOT)]
        hTs = [sb(f"hT{i}", [128, 8, TB], F32) for i in range(2)]
        xn = sb("xn", [128, 8, TB], BF16)
        ubf = sb("ubf", [128, 8, TB + 30], BF16)
        utail = sb("utail", [128, 8, 128], F32)
        f32a = sb("f32a", [128, 4, 1024], F32)
        hmid = sb("hmid", [128, 22, TB], BF16)
        QT = sb("QT", [128, 16, TB], BF16)
        OT = sb("OT", [128, 8, TB], BF16)
        KT = sb("KT", [128, 4, KLEN], BF16)
        Vt = sb("Vt", [128, NVC, 256], BF16)
        kvtok = [sb(f"kvtok{i}", [128, 512], F32) for i in range(2)]
        sq = [sb(f"sq{i}", [128, TB], BF16) for i in range(2)]
        sig = [sb(f"sig{i}", [128, TB], F32) for i in range(2)]
        rstd = sb("rstd", [128, TB], F32)
        mean = sb("mean", [128, TB], F32)
        tmpf = sb("tmpf", [128, TB], F32)
        Ets = [sb(f"Et{i}", [128, 512], F32) for i in range(2)]
        Lt = [sb(f"Lt{i}", [128, 512], BF16) for i in range(3)]
        At = [sb(f"At{i}", [128, 512], BF16) for i in range(3)]
        Sb = [sb(f"Sb{i}", [128, 512], BF16) for i in range(2)]
        vecs = sb("vecs_sb", [128, NV], F32)
        ident = sb("ident", [128, 128], F32)
        cbf = sb("cbf", [128, NCST], BF16)
        ps = es.enter_context(nc.psum_tensor("ps", [128, 8, 512], F32))

        ST0 = 14
        ones_bf = cbf[:, C_ONES:C_ONES + 128]
        ntri_bf = cbf[:, C_NTRI:C_NTRI + 128]
        nones_bf = cbf[:, C_NONES:C_NONES + 128]
        mask_bf = cbf[:, C_MASK:C_MASK + 512]
        mask64_bf = cbf[:, C_MASK64:C_MASK64 + 256]
        allf = [("f32a", q) for q in range(8)]

        def vcol(c):
            return vecs[:, c:c + 1]

        def f8(q):
            return f32a[:, q // 2, (q % 2) * 512:(q % 2) * 512 + 512]

        DMA("sp", vecs[:, :], vecs_d[:, :], (), ["vecs"], "in_vecs")
        DMA("sp", ident[:, :], cst_d[:, C_ID:C_ID + 128], (), ["ident"], "in_id")
        DMA("sp", f32a[:, 0, :], cst_d[:, 0:1024], (), [("f32a", 0), ("f32a", 1)], "in_cst")
        DMA("sp", f32a[:, 1, 0:NCST - 1024], cst_d[:, 1024:NCST], (), [("f32a", 2), ("f32a", 3)], "in_cst2")
        CP("dve", cbf[:, 0:1024], f32a[:, 0, :], [("f32a", 0), ("f32a", 1)], ["cbf"])
        CP("dve", cbf[:, 1024:NCST], f32a[:, 1, 0:NCST - 1024], [("f32a", 2), ("f32a", 3)], ["cbf"])

        def build_diag():
            for i in range(8):
                sid = i % NSLOT
                sl = slots[sid]
                for j in range(31):
                    e = j * 128
                    TSC("dve", sl[:, e // 512, e % 512:e % 512 + 128], cbf[:, C_ID:C_ID + 128], vcol(V_WDW + i * 31 + j), None,
                        ALU.mult, ALU.bypass, ["cbf", "vecs"], [("slot", sid, 0), ("slot", sid, 1)])
                MSET("dve", sl[:, 7, 384:512], 0.0, [("slot", sid, 0), ("slot", sid, 1)])
                DMA("sp", wdg_bf[i], sl[:, 0:8, :], [("slot", sid, 0), ("slot", sid, 1)], [("wdg", i)], ("wdg", sid))

        def prep(name, src2d, dst2d, rows, cols, after=()):
            sv = src2d.rearrange("(p a) n -> p (a n)", p=128)
            dv = dst2d.rearrange("(p a) n -> p (a n)", p=128)
            tot = rows // 128 * cols
            step = 8192
            keys = []
            i = 0
            for off in range(0, tot, step):
                e = min(tot, off + step)
                k = ("wbf", name, i)
                DMA("pool", dv[:, off:e], sv[:, off:e], after, [k], ("prep", name))
                keys.append(k)
                i += 1
            return keys

        WK = {}
        WK["in"] = prep("in", w_in, win_bf, D, 2 * D)
        WK["out"] = prep("out", w_out, wout_bf, D, D)
        WK["gu0"] = prep("gu0", w_gu[0], wgu_bf[0], D, 2 * DFF)
        WK["dn0"] = prep("dn0", w_dn[0], wdn_bf[0], DFF, D)

        def prep_rest():
            WK["kv"] = prep("kv", w_kv, wkv_bf, D, 512)
            WK["q"] = prep("q", w_q, wq_bf, D, D)
            WK["o"] = prep("o", w_o, wo_bf, D, D)
            WK["gu1"] = prep("gu1", w_gu[1], wgu_bf[1], D, 2 * DFF)
            WK["dn1"] = prep("dn1", w_dn[1], wdn_bf[1], DFF, D)

        slot_ctr = [0]

        def load_slab(parts, wkeys):
            sid = slot_ctr[0] % NSLOT
            slot_ctr[0] += 1
            sl = slots[sid]
            for pi, (c0, c1, nk, src) in enumerate(parts):
                wk_ = [("slot", sid, pi)] if len(parts) == 2 else [("slot", sid, 0), ("slot", sid, 1)]
                DMA("sp", sl[:, 0:nk, c0:c1], src, wkeys, wk_, ("slab", sid, pi))
            return sl, sid

        def kc_view(w2d):
            return w2d.rearrange("(kc p) n -> p kc n", p=128)

        SK1 = lambda sid: [("slot", sid, 0), ("slot", sid, 1)]

        def rms_stats(hb, T, bank):
            hT = hTs[hb]
            for i in range(8):
                a = i % 2
                ACT(sq[a][:, :T], hT[:, i, :T], AF.Square, [("hT", hb, i)], [("sq", a)])
                MM(ps[:, bank, :T], ones_bf, sq[a][:, :T], i == 0, i == 7, [("sq", a), "cbf"], [("ps", bank)])
            ACT(tmpf[:, :T], ps[:, bank, :T], AF.Ln, [("ps", bank), "vecs"], ["tmpf"], bias=vcol(V_EPS), scale=1.0 / D)
            ACT(rstd[:, :T], tmpf[:, :T], AF.Exp, ["tmpf"], ["rstd"], scale=-0.5)

        def rms_apply(hb, gbase, T, dst, dkeys):
            hT = hTs[hb]
            for i in range(8):
                STT(dst(i), hT[:, i, :T], vcol(gbase + i), rstd[:, :T], ALU.mult, ALU.mult,
                    [("hT", hb, i), "rstd", "vecs"], dkeys(i))

        def xn_dst(T):
            return (lambda i: xn[:, i, :T]), (lambda i: [("xn", i)])

        def ffn(l, hb, T, banks, exp_only=False):
            hT = hTs[hb]
            rms_stats(hb, T, banks[3])
            d_, k_ = xn_dst(T)
            rms_apply(hb, V_FG + 8 * l, T, d_, k_)
            gv = kc_view(wgu_bf[l])
            for j in range(11):
                sl, sid = load_slab([(0, 256, 8, gv[:, :, j * 256:(j + 1) * 256]),
                                     (256, 512, 8, gv[:, :, DFF + j * 256:DFF + (j + 1) * 256])], WK[f"gu{l}"])
                for jj in range(2):
                    f = 2 * j + jj
                    bA, bB = (banks[0], banks[1]) if f % 2 == 0 else (banks[2], banks[3])
                    for kc in range(8):
                        MM(ps[:, bA, :T], sl[:, kc, jj * 128:(jj + 1) * 128], xn[:, kc, :T], kc == 0, kc == 7,
                           [("slot", sid, 0), ("xn", kc)], [("ps", bA)])
                    for kc in range(8):
                        MM(ps[:, bB, :T], sl[:, kc, 256 + jj * 128:256 + (jj + 1) * 128], xn[:, kc, :T], kc == 0, kc == 7,
                           [("slot", sid, 1), ("xn", kc)], [("ps", bB)])
                    a = f % 2
                    if exp_only:
                        ACT(sig[a][:, :T], ps[:, bA, :T], AF.Exp, [("ps", bA)], [("sig", a)], scale=-1.0)
                        SIGM_FROM_E(sig[a][:, :T], ("sig", a))
                        TT("dve", sig[a][:, :T], ps[:, bA, :T], sig[a][:, :T], ALU.mult, [("ps", bA), ("sig", a)], [("sig", a)])
                    else:
                        ACT(sig[a][:, :T], ps[:, bA, :T], AF.Silu, [("ps", bA)], [("sig", a)])
                    TT("dve", hmid[:, f, :T], sig[a][:, :T], ps[:, bB, :T], ALU.mult, [("sig", a), ("ps", bB)], [("hmid", f)])
            dv = wdn_bf[l].rearrange("(f p) n -> p f n", p=128)
            FG = [(0, 8), (8, 16), (16, 22)]
            for ch in range(2):
                sls = []
                for (f0, f1) in FG:
                    sls.append(load_slab([(0, 512, f1 - f0, dv[:, f0:f1, ch * 512:(ch + 1) * 512])], WK[f"dn{l}"]))
                for j in range(4):
                    mo = 4 * ch + j
                    b = banks[mo % 2]
                    for f in range(NF):
                        gi = f // 8
                        sl, sid = sls[gi]
                        MM(ps[:, b, :T], sl[:, f - 8 * gi, j * 128:(j + 1) * 128], hmid[:, f, :T], f == 0, f == NF - 1,
                           SK1(sid) + [("hmid", f)], [("ps", b)])
                    TT("dve", hT[:, mo, :T], hT[:, mo, :T], ps[:, b, :T], ALU.add, [("hT", hb, mo), ("ps", b)], [("hT", hb, mo)])

        def proj_res(wbf2d, wk, hb, T, src, skeys, banks, bias_base=None):
            hT = hTs[hb]
            v = kc_view(wbf2d)
            for s in range(2):
                sl, sid = load_slab([(0, 512, 8, v[:, :, s * 512:(s + 1) * 512])], wk)
                for j in range(4):
                    mo = 4 * s + j
                    b = banks[mo % 2]
                    for kc in range(8):
                        MM(ps[:, b, :T], sl[:, kc, j * 128:(j + 1) * 128], src(kc), kc == 0, kc == 7,
                           SK1(sid) + skeys(kc), [("ps", b)])
                    if bias_base is not None:
                        STT(hT[:, mo, :T], ps[:, b, :T], vcol(bias_base + mo), hT[:, mo, :T], ALU.add, ALU.add,
                            [("ps", b), ("hT", hb, mo), "vecs"], [("hT", hb, mo)])
                    else:
                        TT("dve", hT[:, mo, :T], hT[:, mo, :T], ps[:, b, :T], ALU.add, [("hT", hb, mo), ("ps", b)], [("hT", hb, mo)])

        def mkblock(kind, si, blk, hb):
            prompt = kind == "p"
            T = TB if prompt else TS
            B = dict(kind=kind, prompt=prompt, si=si, blk=blk, hb=hb, T=T, TT_=min(128, T), NT=max(1, T // 128),
                     t0=blk * TB if prompt else 0)
            if prompt:
                B.update(xsrc=xp[si], ydst=yp[si], kdst=kp[si], vdst=vp[si], csdst=csp[si], s0=blk * TB)
            else:
                B.update(xsrc=xs, ydst=ys, kdst=ks, vdst=vs, csdst=css, s0=PAST)
            return B

        def front(B, hook=None):
            prompt, si, blk, hb, T, TT_, NT, t0 = (B[k] for k in ("prompt", "si", "blk", "hb", "T", "TT_", "NT", "t0"))
            hT = hTs[hb]
            xsrc, csdst = B["xsrc"], B["csdst"]
            if prompt:
                DMA("sp", f32a[:, 0:NT, :], xsrc[t0:t0 + T, :].rearrange("(nt p) d -> p nt d", p=128), (), allf, "in_x")
            else:
                DMA("sp", f32a[0:T, 0, :], xsrc[0:T, :], (), allf, "in_x")
                DMA("sp", f32a[0:30, 1, :], sconv[:, :], (), allf, "in_x")
            for i in range(8):
                b = 4 + i % 2
                for nt in range(NT):
                    TR(ps[:, b, nt * 128:nt * 128 + 128], f32a[:, nt, i * 128:(i + 1) * 128], ident[:, :],
                       allf + ["ident"], [("ps", b)])
                if i % 2 == 0:
                    ACT(hT[:, i, :T], ps[:, b, :T], AF.Copy, [("ps", b)], [("hT", hb, i)])
                else:
                    CP("dve", hT[:, i, :T], ps[:, b, :T], [("ps", b)], [("hT", hb, i)])
            if prompt and blk == 0:
                for i in range(8):
                    MSET("pool", ubf[:, i, 0:30], 0.0, [("uTp", i)])
            if prompt and blk > 0:
                for i in range(8):
                    CP("pool", ubf[:, i, 0:30], ubf[:, i, TB:TB + 30], [("uTm", i)], [("uTp", i)])
            if not prompt:
                for i in range(8):
                    b = 6 + i % 2
                    TR(ps[:, b, 0:128], f32a[:, 1, i * 128:(i + 1) * 128], ident[:, :], allf + ["ident"], [("ps", b)])
                    CP("dve", ubf[:, i, 0:30], ps[:, b, 0:30], [("ps", b)], [("uTp", i)])
            cs_out = (prompt and blk == NBLK - 1) or not prompt
            WT = min(128, T)
            rms_stats(hb, T, 7)
            d_, k_ = xn_dst(T)
            rms_apply(hb, V_AG, T, d_, k_)
            if hook is not None:
                hook()
            wv = kc_view(win_bf)
            for s in range(4):
                sl, sid = load_slab([(0, 256, 8, wv[:, :, s * 256:(s + 1) * 256]),
                                     (256, 512, 8, wv[:, :, D + s * 256:D + (s + 1) * 256])], WK["in"])
                for j in range(2):
                    i = 2 * s + j
                    bA, bB = (4, 5) if i % 2 == 0 else (6, 7)
                    for kc in range(8):
                        MM(ps[:, bA, :T], sl[:, kc, j * 128:(j + 1) * 128], xn[:, kc, :T], kc == 0, kc == 7,
                           [("slot", sid, 0), ("xn", kc)], [("ps", bA)])
                    for kc in range(8):
                        MM(ps[:, bB, :T], sl[:, kc, 256 + j * 128:256 + (j + 1) * 128], xn[:, kc, :T], kc == 0, kc == 7,
                           [("slot", sid, 1), ("xn", kc)], [("ps", bB)])
                    a = i % 2
                    ACT(sig[a][:, :T], ps[:, bB, :T], AF.Exp, [("ps", bB), "vecs"], [("sig", a)], bias=vcol(V_NBG + i), scale=-1.0)
                    SIGM_FROM_E(sig[a][:, :T], ("sig", a))
                    STT(ubf[:, i, 30:30 + T], ps[:, bA, :T], vcol(V_BIN + i), sig[a][:, :T], ALU.add, ALU.mult,
                        [("ps", bA), ("sig", a), "vecs"], [("uTm", i)])
                    if cs_out:
                        STT(utail[:, i, 0:WT], ps[:, bA, T - WT:T], vcol(V_BIN + i), sig[a][:, T - WT:T], ALU.add, ALU.mult,
                            [("ps", bA), ("sig", a), "vecs"], [("utail", i)])
            if cs_out:
                for i in range(8):
                    b = 4 + i // 4
                    TR(ps[:, b, (i % 4) * 128:(i % 4) * 128 + 128], utail[:, i, 0:128], ident[:, :],
                       [("utail", i), "ident"], [("ps", b)])
                for h in range(2):
                    CP("dve", f32a[:, 0, h * 512:(h + 1) * 512], ps[:, 4 + h, :], [("ps", 4 + h)], [("f32a", h)])
                DMA("sp", csdst[:, :], f32a[WT - 30:WT, 0, :], [("f32a", 0), ("f32a", 1)], (), "out_cs", final=True)
            for i in range(8):
                sl, sid = load_slab([(0, 512, 8, wdg_bf[i])], [("wdg", i)])
                b = 4 + i % 2
                for j in range(31):
                    e = j * 128
                    MM(ps[:, b, :T], sl[:, e // 512, e % 512:e % 512 + 128], ubf[:, i, j:j + T], j == 0, j == 30,
                       SK1(sid) + [("uTp", i), ("uTm", i)], [("ps", b)])
                TSC("dve", f8(i)[:, :T], ps[:, b, :T], vcol(V_BDW + i), None, ALU.add, ALU.bypass,
                    [("ps", b), "vecs"], [("f32a", i)])
            for i in range(8):
                ACT(sq[0][:, :T], f8(i)[:, :T], AF.Copy, [("f32a", i)], [("sq", 0)])
                MM(ps[:, 6, :T], ones_bf, sq[0][:, :T], i == 0, i == 7, [("sq", 0), "cbf"], [("ps", 6)])
                ACT(sq[1][:, :T], f8(i)[:, :T], AF.Square, [("f32a", i)], [("sq", 1)])
                MM(ps[:, 7, :T], ones_bf, sq[1][:, :T], i == 0, i == 7, [("sq", 1), "cbf"], [("ps", 7)])
            TSC("dve", mean[:, :T], ps[:, 6, :T], vcol(V_INVD), None, ALU.mult, ALU.bypass, [("ps", 6), "vecs"], ["mean"])
            ACT(tmpf[:, :T], mean[:, :T], AF.Square, ["mean"], ["tmpf"])
            STT(rstd[:, :T], ps[:, 7, :T], vcol(V_INVD), tmpf[:, :T], ALU.mult, ALU.subtract, [("ps", 7), "tmpf", "vecs"], ["rstd"])
            ACT(tmpf[:, :T], rstd[:, :T], AF.Ln, ["rstd", "vecs"], ["tmpf"], bias=vcol(V_EPS), scale=1.0)
            ACT(rstd[:, :T], tmpf[:, :T], AF.Exp, ["tmpf"], ["rstd"], scale=-0.5)
            for i in range(8):
                a = i % 2
                TT("dve", sig[a][:, :T], f8(i)[:, :T], mean[:, :T], ALU.subtract, [("f32a", i), "mean"], [("sig", a)])
                TT("dve", sig[a][:, :T], sig[a][:, :T], rstd[:, :T], ALU.mult, [("sig", a), "rstd"], [("sig", a)])
                TSC("dve", sig[a][:, :T], sig[a][:, :T], vcol(V_LNG + i), vcol(V_LNB + i), ALU.mult, ALU.add,
                    [("sig", a), "vecs"], [("sig", a)])
                et, ek = (tmpf, "tmpf") if a == 0 else (kvtok[0], ("kvtok", 0))
                ACT(et[:, :T], sig[a][:, :T], AF.Exp, [("sig", a)], [ek], scale=-1.0)
                SIGM_FROM_E(et[:, :T], ek)
                TT("dve", hmid[:, ST0 + i, :T], sig[a][:, :T], et[:, :T], ALU.mult, [("sig", a), ek], [("hmid", ST0 + i)])
            proj_res(wout_bf, WK["out"], hb, T, lambda kc: hmid[:, ST0 + kc, :T], lambda kc: [("hmid", ST0 + kc)],
                     (4, 5), bias_base=V_BOUT)
            ffn(0, hb, T, (4, 5, 6, 7), exp_only=True)

        def load_cache():
            for c in range(NPC):
                DMA("pool", hmid[:, c // 2, (c % 2) * 256:(c % 2) * 256 + 256], ck[c * 128:(c + 1) * 128, :], (),
                    [("hmid", c // 2)], ("in_ck", c % 2))
            DMA("pool", Vt[:, 0:NPC, :], cv.rearrange("(c p) n -> p c n", p=128), (), [("Vt", c) for c in range(NPC)], "in_cv")
            for c in range(NPC):
                b = c % 2
                for g in range(4):
                    MM(ps[0:64, b, g * 128:(g + 1) * 128], hmid[:, c // 2, (c % 2) * 256 + g * 64:(c % 2) * 256 + (g + 1) * 64],
                       cbf[:, C_ID:C_ID + 128], g == 0, g == 3, [("hmid", c // 2), "cbf"], [("ps", b)])
                for g in range(4):
                    if g % 2 == 0:
                        ACT(KT[0:64, g, c * 128:(c + 1) * 128], ps[0:64, b, g * 128:(g + 1) * 128], AF.Copy, [("ps", b)], [("KT", c // 4, g)])
                    else:
                        CP("dve", KT[0:64, g, c * 128:(c + 1) * 128], ps[0:64, b, g * 128:(g + 1) * 128], [("ps", b)], [("KT", c // 4, g)])

        def qidx(hd):
            r = hd % 4
            return 4 * (hd // 4) + (r % 2) * 2 + r // 2

        def kv_part(B, banks):
            prompt, si, blk, hb, T, TT_, NT, t0, s0 = (B[k] for k in ("prompt", "si", "blk", "hb", "T", "TT_", "NT", "t0", "s0"))
            kdst, vdst = B["kdst"], B["vdst"]
            rms_stats(hb, T, banks[3])
            d_, k_ = xn_dst(T)
            rms_apply(hb, V_KVG, T, d_, k_)
            sl, sid = load_slab([(0, 512, 8, kc_view(wkv_bf)[:, :, :])], WK["kv"])
            for nt in range(NT):
                b = banks[nt % 2]
                a = nt % 2
                for kc in range(8):
                    MM(ps[0:TT_, b, :], xn[:, kc, nt * 128:nt * 128 + TT_], sl[:, kc, :], kc == 0, kc == 7,
                       SK1(sid) + [("xn", kc)], [("ps", b)])
                ACT(kvtok[a][0:TT_, :], ps[0:TT_, b, :], AF.Copy, [("ps", b)], [("kvtok", a)])
                vc = (s0 + nt * 128) // 128
                CP("dve", Vt[0:TT_, vc, :], ps[0:TT_, b, 256:512], [("ps", b)], [("Vt", vc)])
                r0 = t0 + nt * 128 if prompt else 0
                DMA("sp", kdst[r0:r0 + TT_, :], kvtok[a][0:TT_, 0:256], [("kvtok", a)], (), ("out_kv", a), final=True)
                DMA("sp", vdst[r0:r0 + TT_, :], kvtok[a][0:TT_, 256:512], [("kvtok", a)], (), ("out_kv", a), final=True)
            kblk = ("KT", blk if prompt else "s")
            for g in range(4):
                b = banks[2 + g % 2]
                for kc in range(8):
                    MM(ps[0:64, b, :T], sl[:, kc, g * 64:(g + 1) * 64], xn[:, kc, :T], kc == 0, kc == 7,
                       SK1(sid) + [("xn", kc)], [("ps", b)])
                ACT(KT[0:64, g, s0:s0 + T], ps[0:64, b, :T], AF.Copy, [("ps", b)], [kblk + (g,)])

        def q_part(B):
            hb, T = B["hb"], B["T"]
            d_, k_ = xn_dst(T)
            rms_apply(hb, V_BG, T, d_, k_)
            qv = kc_view(wq_bf)
            for s in range(2):
                sl, sid = load_slab([(0, 512, 8, qv[:, :, s * 512:(s + 1) * 512])], WK["q"])
                for hh in range(8):
                    hd = 8 * s + hh
                    b = 4 + hd % 2
                    for kc in range(8):
                        MM(ps[0:64, b, :T], sl[:, kc, hh * 64:(hh + 1) * 64], xn[:, kc, :T], kc == 0, kc == 7,
                           SK1(sid) + [("xn", kc)], [("ps", b)])
                    qi = qidx(hd)
                    ACT(QT[0:64, qi, :T], ps[0:64, b, :T], AF.Copy, [("ps", b)], [("QT", qi)], scale=0.125)

        def attn(B):
            prompt, si, blk, T, TT_, NT, t0 = (B[k] for k in ("prompt", "si", "blk", "T", "TT_", "NT", "t0"))
            TQ = TT_
            W4 = 4 * TQ
            units = []
            for g in range(4):
                for qbl in range(NT):
                    if prompt:
                        last = (t0 // 128) + qbl
                        cl = [(c * 128, 128, c == last, ("KT", c // 4, g), c) for c in range(last, -1, -1)]
                    else:
                        cl = [(PAST, TS, True, ("KT", "s", g), NPC)]
                        cl += [(c * 128, 128, False, ("KT", c // 4, g), c) for c in range(NPC - 1, -1, -1)]
                    for ci, (koff, kn, diag, kkey, vc) in enumerate(cl):
                        units.append(dict(g=g, qbl=qbl, koff=koff, kn=kn, diag=diag, kkey=kkey, vc=vc,
                                          first=ci == 0, lastc=ci == len(cl) - 1, carry=None))
            NU = len(units)
            mk = mask_bf if TQ == 128 else mask64_bf
            scur = [0]

            def stageZ(k):
                u = units[k]; p = k % 3; kn = u["kn"]; g = u["g"]
                q0 = u["qbl"] * 128
                MM(ps[0:kn, p, 0:W4], KT[0:64, g, u["koff"]:u["koff"] + kn], QT[0:64, 4 * g:4 * g + 4, q0:q0 + TQ],
                   True, True, [u["kkey"]] + [("QT", 4 * g + x) for x in range(4)], [("ps", p)])
                ACT(Ets[k % 2][0:kn, :W4], ps[0:kn, p, :W4], AF.Exp, [("ps", p)], [("Et", k % 2)])

            def stageL(k):
                u = units[k]; l = k % 3; kn = u["kn"]
                ACT(Lt[l][0:kn, :W4], Ets[k % 2][0:kn, :W4], AF.Ln, [("Et", k % 2), "vecs"], [("Lt", l)],
                    bias=vecs[0:kn, V_ONE:V_ONE + 1], scale=1.0)
                if u["diag"]:
                    TT("pool", Lt[l][0:kn, :W4], Lt[l][0:kn, :W4], mk[0:kn, :W4], ALU.mult, [("Lt", l), "cbf"], [("Lt", l)])
                if u["first"]:
                    u["carry"] = None
                    if not u["lastc"]:
                        scur[0] = 0
                        if kn < 128:
                            MSET("pool", Sb[0][:, :W4], 0.0, [("Sb", 0)])
                        CP("pool", Sb[0][0:kn, :W4], Lt[l][0:kn, :W4], [("Lt", l)], [("Sb", 0)])
                else:
                    u["carry"] = scur[0]
                    if not u["lastc"]:
                        o = scur[0]; n = 1 - o
                        TT("pool", Sb[n][:, :W4], Sb[o][:, :W4], Lt[l][:, :W4], ALU.add, [("Sb", o), ("Lt", l)], [("Sb", n)])
                        scur[0] = n

            def stageT(k):
                u = units[k]; p = k % 3; l = k % 3; kn = u["kn"]
                MM(ps[0:kn, p, :W4], ntri_bf[0:kn, 0:kn], Lt[l][0:kn, :W4], False, u["first"], [("Lt", l), "cbf"], [("ps", p)])
                if not u["first"]:
                    c = u["carry"]
                    MM(ps[0:kn, p, :W4], nones_bf[:, 0:kn], Sb[c][:, :W4], False, True, [("Sb", c), "cbf"], [("ps", p)])
                a = k % 3
                ACT(At[a][0:kn, :W4], ps[0:kn, p, :W4], AF.Exp, [("ps", p)], [("At", a)])
                if u["diag"]:
                    TT("pool", At[a][0:kn, :W4], At[a][0:kn, :W4], mk[0:kn, :W4], ALU.mult, [("At", a), "cbf"], [("At", a)])

            def stageV(k):
                u = units[k]; a = k % 3; kn = u["kn"]; g = u["g"]
                ob = 3
                for rl in range(2):
                    MM(ps[rl * 64:(rl + 1) * 64, ob, 0:2 * TQ], Vt[0:kn, u["vc"], g * 64:(g + 1) * 64],
                       At[a][0:kn, rl * 2 * TQ:(rl + 1) * 2 * TQ], u["first"], u["lastc"],
                       [("Vt", u["vc"]), ("At", a)], [("ps", ob)])
                if u["lastc"]:
                    for rh in range(2):
                        oc = 2 * g + rh
                        dst = OT[:, oc, u["qbl"] * 128:u["qbl"] * 128 + TQ]
                        ACT(dst, ps[:, ob, rh * TQ:(rh + 1) * TQ], AF.Copy, [("ps", ob)], [("OT", oc)])

            for k in range(NU + 2):
                if k < NU:
                    stageZ(k)
                if 0 <= k - 1 < NU:
                    stageT(k - 1)
                if k < NU:
                    stageL(k)
                if 0 <= k - 2 < NU:
                    stageV(k - 2)

        def wo(B):
            hb, T = B["hb"], B["T"]
            proj_res(wo_bf, WK["o"], hb, T, lambda kc: OT[:, kc, :T], lambda kc: [("OT", kc)], (4, 5))

        def tail2(B):
            prompt, hb, T, TT_, NT, t0 = (B[k] for k in ("prompt", "hb", "T", "TT_", "NT", "t0"))
            ydst = B["ydst"]
            ffn(1, hb, T, (4, 5, 6, 7), exp_only=True)
            rms_stats(hb, T, 7)
            hT = hTs[hb]
            rms_apply(hb, V_FIN, T, lambda i: hT[:, i, 0:T], lambda i: [("hT", hb, i)])
            for nt in range(NT):
                for h in range(2):
                    b = 4 + h
                    for ii in range(4):
                        i = 4 * h + ii
                        TR(ps[:, b, ii * 128:(ii + 1) * 128], hT[:, i, nt * 128:nt * 128 + 128], ident[:, :],
                           [("hT", hb, i), "ident"], [("ps", b)])
                    if h == 0:
                        ACT(f32a[0:TT_, nt, 0:512], ps[0:TT_, b, :], AF.Copy, [("ps", b)], [("f32a", 2 * nt)])
                    else:
                        CP("dve", f32a[0:TT_, nt, 512:1024], ps[0:TT_, b, :], [("ps", b)], [("f32a", 2 * nt + 1)])
            if prompt:
                DMA("sp", ydst[t0:t0 + T, :].rearrange("(nt p) d -> p nt d", p=128), f32a[:, 0:NT, :], allf, (), "out_y", final=True)
            else:
                DMA("sp", ydst[0:T, :], f32a[0:T, 0, :], allf, (), "out_y", final=True)

        blocks = []
        for si in range(NP):
            for blk in range(NBLK):
                blocks.append(mkblock("p", si, blk, len(blocks) % 2))
        blocks.append(mkblock("s", 0, 0, len(blocks) % 2))
        NB = len(blocks)

        front(blocks[0], hook=build_diag)
        prep_rest()
        kv_part(blocks[0], (4, 5, 6, 7))
        q_part(blocks[0])
        la, lb = [], []
        with S.thread(la):
            attn(blocks[0])
        if NB > 1:
            with S.thread(lb):
                front(blocks[1])
                kv_part(blocks[1], (4, 5, 6, 7))
        S.merge(la, lb)
        for bi in range(NB):
            B = blocks[bi]
            wo(B)
            Bn = blocks[bi + 1] if bi + 1 < NB else None
            if Bn is not None:
                if not Bn["prompt"]:
                    load_cache()
                q_part(Bn)
            la, lb = [], []
            if Bn is not None:
                with S.thread(la):
                    attn(Bn)
            with S.thread(lb):
                tail2(B)
                if bi + 2 < NB:
                    front(blocks[bi + 2])
                    kv_part(blocks[bi + 2], (4, 5, 6, 7))
            S.merge(la, lb)
        cnt = S.emit()
    return nc, cnt


def pack_vecs(a_norm_g, conv_b_in, conv_w_dw, conv_b_dw, conv_ln_g, conv_ln_b, conv_b_out,
              kv_norm_g, b_norm_g, ffn_norm_g, final_norm_g):
    v = np.zeros((128, NV), np.float32)

    def put(base, vec):
        vec = np.asarray(vec, np.float32).reshape(-1, 128)
        v[:, base:base + vec.shape[0]] = vec.T

    put(V_AG, a_norm_g[0]); put(V_BIN, conv_b_in[0])
    wd = np.asarray(conv_w_dw[0], np.float32)
    for i in range(8):
        v[:, V_WDW + i * 31:V_WDW + (i + 1) * 31] = wd[:, i * 128:(i + 1) * 128].T
    put(V_BDW, conv_b_dw[0]); put(V_LNG, conv_ln_g[0]); put(V_LNB, conv_ln_b[0]); put(V_BOUT, conv_b_out[0])
    put(V_KVG, kv_norm_g); put(V_BG, b_norm_g[0]); put(V_FG, ffn_norm_g[0]); put(V_FG + 8, ffn_norm_g[1])
    put(V_FIN, final_norm_g)
    v[:, V_EPS] = EPS
    v[:, V_ONE] = 1.0
    v[:, V_INVD] = 1.0 / D
    v[:, V_NBG:V_NBG + 8] = -v[:, V_BIN + 8:V_BIN + 16]
    return v


def make_cst():
    c = np.zeros((128, NCST), np.float32)
    ar = np.arange(128)
    c[:, C_ID:C_ID + 128] = np.eye(128)
    c[:, C_ONES:C_ONES + 128] = 1.0
    c[:, C_NTRI:C_NTRI + 128] = -1.0 * (ar[:, None] >= ar[None, :])
    c[:, C_NONES:C_NONES + 128] = -1.0
    m = (ar[:, None] < ar[None, :]).astype(np.float32)
    c[:, C_MASK:C_MASK + 512] = np.tile(m, (1, 4))
    c[:, C_MASK64:C_MASK64 + 256] = np.tile(m[:, :64], (1, 4))
    return c


def run(inputs, n_cores, NP, runner=None):
    f = lambda k: np.ascontiguousarray(np.asarray(inputs[k], np.float32))
    x_prompt, x_sample = f("x_prompt"), f("x_sample")
    SEQ = x_prompt.shape[1]; TS = x_sample.shape[1]; PAST = inputs["cache_k"].shape[1]
    nc, cnt = build(NP, SEQ, PAST, TS)
    vecs = pack_vecs(*[f(k) for k in ("a_norm_g", "conv_b_in", "conv_w_dw", "conv_b_dw", "conv_ln_g", "conv_ln_b",
                                      "conv_b_out", "kv_norm_g", "b_norm_g", "ffn_norm_g", "final_norm_g")])
    cst = make_cst()
    shared = dict(w_in=f("conv_w_in")[0], w_out=f("conv_w_out")[0], w_kv=f("w_kv"), w_q=f("w_q")[0], w_o=f("w_o")[0],
                  w_gu=f("ffn_w_gu"), w_dn=f("ffn_w_down"), vecs=vecs, cst=cst)
    sc, ckf, cvf = f("state_conv"), f("cache_k"), f("cache_v")
    in_maps = []
    for c in range(n_cores):
        m = dict(shared)
        m["xp"] = np.ascontiguousarray(x_prompt[c * NP:(c + 1) * NP])
        m["xs"] = np.ascontiguousarray(x_sample[c])
        m["sconv"] = np.ascontiguousarray(sc[0, c])
        m["ck"] = np.ascontiguousarray(ckf[c].reshape(PAST, 256))
        m["cv"] = np.ascontiguousarray(cvf[c].reshape(PAST, 256))
        in_maps.append(m)
    if runner is None:
        res = run_bass_kernel_spmd(nc, in_maps, core_ids=list(range(n_cores))).results
    else:
        res = runner(nc, in_maps)
    cat = lambda k: np.concatenate([np.asarray(r[k], np.float32) for r in res], axis=0)
    stk = lambda k: np.stack([np.asarray(r[k], np.float32) for r in res], axis=0)
    B = n_cores * NP
    y_prompt = cat("yp")
    y_sample = stk("ys")
    csp = cat("csp")[None]
    k_prompt = cat("kp").reshape(B, SEQ, 4, 64)
    v_prompt = cat("vp").reshape(B, SEQ, 4, 64)
    css = stk("css")[None]
    k_sample = stk("ks").reshape(n_cores, TS, 4, 64)
    v_sample = stk("vs").reshape(n_cores, TS, 4, 64)
    return (y_prompt, y_sample, csp, k_prompt, v_prompt, css, k_sample, v_sample)


def kernel(**inputs):
    return run(inputs, 8, 2)
```

```python
import contextlib
import numpy as np
import concourse.bass as bass
import concourse.mybir as mybir
from concourse.bass_utils import run_bass_kernel_spmd

F32 = mybir.dt.float32
BF16 = mybir.dt.bfloat16
AF = mybir.ActivationFunctionType
ALU = mybir.AluOpType
ENG = ("pe", "act", "dve", "pool", "sp")

D = 1024
DFF = 2816
NF = 22
EPS = 1e-6


class Op:
    __slots__ = ("eng", "fn", "reads", "writes", "dma", "idx", "waits", "signal", "count", "sem", "deps", "cost")

    def __init__(self, eng, fn, reads, writes, dma, final, cost=0.0):
        self.eng = eng; self.fn = fn; self.reads = tuple(reads); self.cost = cost
        self.writes = tuple(writes) + tuple(k for k in self.reads
                                            if isinstance(k, tuple) and k and k[0] == "ps" and k not in writes)
        self.dma = dma; self.waits = {}; self.count = None; self.sem = None
        self.signal = bool(final) or (dma is not None)


class Sched:
    def __init__(self, nc):
        self.nc = nc
        self.ops = []
        self.cur = self.ops

    def op(self, eng, fn, reads=(), writes=(), dma=None, final=False, cost=0.0):
        o = Op(eng, fn, reads, writes, dma, final, cost)
        self.cur.append(o)
        return o

    @contextlib.contextmanager
    def thread(self, lst):
        prev = self.cur
        self.cur = lst
        try:
            yield lst
        finally:
            self.cur = prev

    def merge(self, A, B):
        def fracs(L):
            tot = sum(o.cost for o in L) or 1.0
            acc = 0.0
            out = []
            for o in L:
                out.append(acc / tot)
                acc += o.cost
            return out
        fa, fb = fracs(A), fracs(B)
        i = j = 0
        la, lb = len(A), len(B)
        while i < la or j < lb:
            if j >= lb or (i < la and fa[i] <= fb[j]):
                self.cur.append(A[i]); i += 1
            else:
                self.cur.append(B[j]); j += 1

    def analyze(self):
        lastw = {}
        readers = {}
        for idx, o in enumerate(self.ops):
            o.idx = idx
            deps = {}
            for k in o.reads:
                w = lastw.get(k)
                if w is not None:
                    deps[w.idx] = "raw"
            for k in o.writes:
                w = lastw.get(k)
                if w is not None and w.idx not in deps:
                    deps[w.idx] = "waw"
                for r in readers.get(k, ()):
                    if r.idx not in deps and r.idx != o.idx:
                        deps[r.idx] = "war"
            for k in o.reads:
                readers.setdefault(k, []).append(o)
            for k in o.writes:
                lastw[k] = o
                readers[k] = []
            o.deps = deps

    def finalize(self):
        self.analyze()
        ops = self.ops
        need = []
        for o in ops:
            for di, kind in o.deps.items():
                d = ops[di]
                if d.dma is None and o.dma is None and d.eng == o.eng and d.eng == "pe":
                    continue
                need.append((o, d))
                d.signal = True
        cnt = {}
        for o in ops:
            if not o.signal:
                continue
            if o.dma is not None:
                key = ("dma", o.dma)
                cnt[key] = cnt.get(key, 0) + 16
            else:
                key = ("eng", o.eng)
                cnt[key] = cnt.get(key, 0) + 1
            o.sem = key
            o.count = cnt[key]
        for o, d in need:
            prev = o.waits.get(d.sem, 0)
            if d.count > prev:
                o.waits[d.sem] = d.count
        self.semkeys = sorted(cnt.keys(), key=str)
        self.final_counts = {k: v for k, v in cnt.items() if k[0] == "dma"}
        return cnt

    def emit(self):
        nc = self.nc
        cnt = self.finalize()
        streams = {e: [] for e in ENG}
        for o in self.ops:
            streams[o.eng].append(o)
        with contextlib.ExitStack() as es:
            sems = {}
            for i, k in enumerate(self.semkeys):
                sems[k] = es.enter_context(nc.semaphore(f"s{i}"))
            block = es.enter_context(nc.Block())

            def run(engname):
                def body(eng):
                    waited = {}
                    for o in streams[engname]:
                        for sk, v in o.waits.items():
                            if waited.get(sk, 0) < v:
                                eng.wait_ge(sems[sk], v)
                                waited[sk] = v
                        ins = o.fn(eng)
                        if o.signal:
                            ins.then_inc(sems[o.sem], 16 if o.dma is not None else 1)
                    if engname == "sp":
                        for sk, v in self.final_counts.items():
                            if waited.get(sk, 0) < v:
                                eng.wait_ge(sems[sk], v)
                return body

            block.tensor(run("pe"))
            block.scalar(run("act"))
            block.vector(run("dve"))
            block.gpsimd(run("pool"))
            block.sync(run("sp"))
        return cnt


V_AG = 0; V_BIN = 8; V_WDW = 24; V_BDW = 272; V_LNG = 280; V_LNB = 288; V_BOUT = 296
V_KVG = 304; V_BG = 312; V_FG = 320; V_FIN = 336; V_EPS = 344; V_ONE = 345; V_INVD = 346; V_NBG = 347; NV = 355
C_ID = 0; C_ONES = 128; C_NTRI = 256; C_NONES = 384; C_MASK = 512; C_MASK64 = 1024; NCST = 1280


def build(NP, SEQ, PAST, TS):
    nc = bass.Bass("TRN2", target_bir_lowering=False)
    TB = 512
    NBLK = SEQ // TB
    NPC = PAST // 128
    KLEN = max(SEQ, PAST + TS)
    NVC = max(SEQ // 128, NPC + 1)

    def din(name, shape, dt=F32):
        return nc.dram_tensor(name, shape, dt, kind="ExternalInput").ap()

    def dout(name, shape):
        return nc.dram_tensor(name, shape, F32, kind="ExternalOutput").ap()

    def dscr(name, shape):
        return nc.dram_tensor(name, shape, BF16, kind="Internal").ap()

    xp = din("xp", [NP, SEQ, D]); xs = din("xs", [TS, D]); sconv = din("sconv", [30, D])
    ck = din("ck", [PAST, 256]); cv = din("cv", [PAST, 256])
    w_in = din("w_in", [D, 2 * D]); w_out = din("w_out", [D, D]); w_kv = din("w_kv", [D, 512])
    w_q = din("w_q", [D, D]); w_o = din("w_o", [D, D])
    w_gu = din("w_gu", [2, D, 2 * DFF]); w_dn = din("w_dn", [2, DFF, D])
    vecs_d = din("vecs", [128, NV]); cst_d = din("cst", [128, NCST])
    yp = dout("yp", [NP, SEQ, D]); ys = dout("ys", [TS, D]); csp = dout("csp", [NP, 30, D])
    kp = dout("kp", [NP, SEQ, 256]); vp = dout("vp", [NP, SEQ, 256]); css = dout("css", [30, D])
    ks = dout("ks", [TS, 256]); vs = dout("vs", [TS, 256])
    win_bf = dscr("win_bf", [D, 2 * D]); wout_bf = dscr("wout_bf", [D, D]); wkv_bf = dscr("wkv_bf", [D, 512])
    wq_bf = dscr("wq_bf", [D, D]); wo_bf = dscr("wo_bf", [D, D])
    wgu_bf = dscr("wgu_bf", [2, D, 2 * DFF]); wdn_bf = dscr("wdn_bf", [2, DFF, D])
    wdg_bf = dscr("wdg_bf", [8, 128, 8, 512])

    S = Sched(nc)

    def ncols(ap):
        n = 1
        for d in list(ap.shape)[1:]:
            n *= int(d)
        return n

    def MM(out, lhsT, rhs, start, stop, r, w):
        S.op("pe", lambda e: e.matmul(out, lhsT=lhsT, rhs=rhs, start=start, stop=stop, skip_group_check=True), r, w,
             cost=ncols(rhs) / 2.4 + 8)

    def TR(out, in_, ident, r, w):
        S.op("pe", lambda e: e.transpose(out, in_, ident), r, w, cost=250.0)

    def ACT(out, in_, func, r, w, bias=None, scale=None):
        kw = {}
        if bias is not None:
            kw["bias"] = bias
        if scale is not None:
            kw["scale"] = scale
        S.op("act", lambda e: e.activation(out=out, in_=in_, func=func, **kw), r, w, cost=150 + ncols(out) / 1.3)

    def TT(eng, out, in0, in1, op, r, w):
        c = 120 + ncols(out) / 0.96
        S.op(eng, lambda e: e.tensor_tensor(out=out, in0=in0, in1=in1, op=op), r, w, cost=c * (2.0 if eng == "pool" else 1.0))

    def TSC(eng, out, in0, s1, s2, op0, op1, r, w):
        S.op(eng, lambda e: e.tensor_scalar(out=out, in0=in0, scalar1=s1, scalar2=s2, op0=op0, op1=op1), r, w,
             cost=120 + ncols(out) / 0.96)

    def STT(out, in0, scalar, in1, op0, op1, r, w):
        S.op("dve", lambda e: e.scalar_tensor_tensor(out=out, in0=in0, scalar=scalar, in1=in1, op0=op0, op1=op1), r, w,
             cost=120 + ncols(out) / 0.96)

    def SIGM_FROM_E(t, key):
        ACT(t, t, AF.Ln, [key, "vecs"], [key], bias=vecs[:, V_ONE:V_ONE + 1], scale=1.0)
        ACT(t, t, AF.Exp, [key], [key], scale=-1.0)

    def CP(eng, out, in_, r, w):
        c = 120 + ncols(out) / 0.96
        S.op(eng, lambda e: e.tensor_copy(out=out, in_=in_), r, w, cost=c * (2.0 if eng == "pool" else 1.0))

    def MSET(eng, ap, val, w):
        S.op(eng, lambda e: e.memset(ap, val), (), w, cost=100.0)

    def DMA(eng, out, in_, r, w, group, final=False):
        S.op(eng, lambda e: e.dma_start(out=out, in_=in_), r, w, dma=group, final=final)

    with contextlib.ExitStack() as es:
        def sb(name, shape, dt):
            return es.enter_context(nc.sbuf_tensor(name, shape, dt))

        NSLOT = 4
        slots = [sb(f"slot{i}", [128, 8, 512], BF16) for i in range(NSLOT)]
        hTs = [sb(f"hT{i}", [128, 8, TB], F32) for i in range(2)]
        xn = sb("xn", [128, 8, TB], BF16)
        ubf = sb("ubf", [128, 8, TB + 30], BF16)
        utail = sb("utail", [128, 8, 128], F32)
        f32a = sb("f32a", [128, 4, 1024], F32)
        hmid = sb("hmid", [128, 22, TB], BF16)
        QT = sb("QT", [128, 16, TB], BF16)
        OT = sb("OT", [128, 8, TB], BF16)
        KT = sb("KT", [128, 4, KLEN], BF16)
        Vt = sb("Vt", [128, NVC, 256], BF16)
        kvtok = [sb(f"kvtok{i}", [128, 512], F32) for i in range(2)]
        sq = [sb(f"sq{i}", [128, TB], BF16) for i in range(2)]
        sig = [sb(f"sig{i}", [128, TB], F32) for i in range(2)]
        rstd = sb("rstd", [128, TB], F32)
        mean = sb("mean", [128, TB], F32)
        tmpf = sb("tmpf", [128, TB], F32)
        Ets = [sb(f"Et{i}", [128, 512], F32) for i in range(2)]
        Lt = [sb(f"Lt{i}", [128, 512], BF16) for i in range(3)]
        At = [sb(f"At{i}", [128, 512], BF16) for i in range(3)]
        Sb = [sb(f"Sb{i}", [128, 512], BF16) for i in range(2)]
        vecs = sb("vecs_sb", [128, NV], F32)
        ident = sb("ident", [128, 128], F32)
        cbf = sb("cbf", [128, NCST], BF16)
        ps = es.enter_context(nc.psum_tensor("ps", [128, 8, 512], F32))

        ST0 = 14
        ones_bf = cbf[:, C_ONES:C_ONES + 128]
        ntri_bf = cbf[:, C_NTRI:C_NTRI + 128]
        nones_bf = cbf[:, C_NONES:C_NONES + 128]
        mask_bf = cbf[:, C_MASK:C_MASK + 512]
        mask64_bf = cbf[:, C_MASK64:C_MASK64 + 256]
        allf = [("f32a", q) for q in range(8)]

        def vcol(c):
            return vecs[:, c:c + 1]

        def f8(q):
            return f32a[:, q // 2, (q % 2) * 512:(q % 2) * 512 + 512]

        DMA("sp", vecs[:, :], vecs_d[:, :], (), ["vecs"], "in_vecs")
        DMA("sp", ident[:, :], cst_d[:, C_ID:C_ID + 128], (), ["ident"], "in_id")
        DMA("sp", f32a[:, 0, :], cst_d[:, 0:1024], (), [("f32a", 0), ("f32a", 1)], "in_cst")
        DMA("sp", f32a[:, 1, 0:NCST - 1024], cst_d[:, 1024:NCST], (), [("f32a", 2), ("f32a", 3)], "in_cst2")
        CP("dve", cbf[:, 0:1024], f32a[:, 0, :], [("f32a", 0), ("f32a", 1)], ["cbf"])
        CP("dve", cbf[:, 1024:NCST], f32a[:, 1, 0:NCST - 1024], [("f32a", 2), ("f32a", 3)], ["cbf"])

        def build_diag():
            for i in range(8):
                sid = i % NSLOT
                sl = slots[sid]
                for j in range(31):
                    e = j * 128
                    TSC("dve", sl[:, e // 512, e % 512:e % 512 + 128], cbf[:, C_ID:C_ID + 128], vcol(V_WDW + i * 31 + j), None,
                        ALU.mult, ALU.bypass, ["cbf", "vecs"], [("slot", sid, 0), ("slot", sid, 1)])
                MSET("dve", sl[:, 7, 384:512], 0.0, [("slot", sid, 0), ("slot", sid, 1)])
                DMA("sp", wdg_bf[i], sl[:, 0:8, :], [("slot", sid, 0), ("slot", sid, 1)], [("wdg", i)], ("wdg", sid))

        def prep(name, src2d, dst2d, rows, cols, after=()):
            sv = src2d.rearrange("(p a) n -> p (a n)", p=128)
            dv = dst2d.rearrange("(p a) n -> p (a n)", p=128)
            tot = rows // 128 * cols
            step = 8192
            keys = []
            i = 0
            for off in range(0, tot, step):
                e = min(tot, off + step)
                k = ("wbf", name, i)
                DMA("pool", dv[:, off:e], sv[:, off:e], after, [k], ("prep", name))
                keys.append(k)
                i += 1
            return keys

        WK = {}
        WK["in"] = prep("in", w_in, win_bf, D, 2 * D)
        WK["out"] = prep("out", w_out, wout_bf, D, D)
        WK["gu0"] = prep("gu0", w_gu[0], wgu_bf[0], D, 2 * DFF)
        WK["dn0"] = prep("dn0", w_dn[0], wdn_bf[0], DFF, D)

        def prep_rest():
            WK["kv"] = prep("kv", w_kv, wkv_bf, D, 512)
            WK["q"] = prep("q", w_q, wq_bf, D, D)
            WK["o"] = prep("o", w_o, wo_bf, D, D)
            WK["gu1"] = prep("gu1", w_gu[1], wgu_bf[1], D, 2 * DFF)
            WK["dn1"] = prep("dn1", w_dn[1], wdn_bf[1], DFF, D)

        slot_ctr = [0]

        def load_slab(parts, wkeys):
            sid = slot_ctr[0] % NSLOT
            slot_ctr[0] += 1
            sl = slots[sid]
            for pi, (c0, c1, nk, src) in enumerate(parts):
                wk_ = [("slot", sid, pi)] if len(parts) == 2 else [("slot", sid, 0), ("slot", sid, 1)]
                DMA("sp", sl[:, 0:nk, c0:c1], src, wkeys, wk_, ("slab", sid, pi))
            return sl, sid

        def kc_view(w2d):
            return w2d.rearrange("(kc p) n -> p kc n", p=128)

        SK1 = lambda sid: [("slot", sid, 0), ("slot", sid, 1)]

        def rms_stats(hb, T, bank):
            hT = hTs[hb]
            for i in range(8):
                a = i % 2
                if a == 0:
                    ACT(sq[a][:, :T], hT[:, i, :T], AF.Square, [("hT", hb, i)], [("sq", a)])
                else:
                    TT("dve", sq[a][:, :T], hT[:, i, :T], hT[:, i, :T], ALU.mult, [("hT", hb, i)], [("sq", a)])
                MM(ps[:, bank, :T], ones_bf, sq[a][:, :T], i == 0, i == 7, [("sq", a), "cbf"], [("ps", bank)])
            ACT(tmpf[:, :T], ps[:, bank, :T], AF.Ln, [("ps", bank), "vecs"], ["tmpf"], bias=vcol(V_EPS), scale=1.0 / D)
            ACT(rstd[:, :T], tmpf[:, :T], AF.Exp, ["tmpf"], ["rstd"], scale=-0.5)

        def rms_apply(hb, gbase, T, dst, dkeys):
            hT = hTs[hb]
            for i in range(8):
                STT(dst(i), hT[:, i, :T], vcol(gbase + i), rstd[:, :T], ALU.mult, ALU.mult,
                    [("hT", hb, i), "rstd", "vecs"], dkeys(i))

        def xn_dst(T):
            return (lambda i: xn[:, i, :T]), (lambda i: [("xn", i)])

        def ffn(l, hb, T, banks, exp_only=False):
            hT = hTs[hb]
            rms_stats(hb, T, banks[3])
            d_, k_ = xn_dst(T)
            rms_apply(hb, V_FG + 8 * l, T, d_, k_)
            gv = kc_view(wgu_bf[l])
            for j in range(11):
                sl, sid = load_slab([(0, 256, 8, gv[:, :, j * 256:(j + 1) * 256]),
                                     (256, 512, 8, gv[:, :, DFF + j * 256:DFF + (j + 1) * 256])], WK[f"gu{l}"])
                for jj in range(2):
                    f = 2 * j + jj
                    bA, bB = (banks[0], banks[1]) if f % 2 == 0 else (banks[2], banks[3])
                    for kc in range(8):
                        MM(ps[:, bA, :T], sl[:, kc, jj * 128:(jj + 1) * 128], xn[:, kc, :T], kc == 0, kc == 7,
                           [("slot", sid, 0), ("xn", kc)], [("ps", bA)])
                    for kc in range(8):
                        MM(ps[:, bB, :T], sl[:, kc, 256 + jj * 128:256 + (jj + 1) * 128], xn[:, kc, :T], kc == 0, kc == 7,
                           [("slot", sid, 1), ("xn", kc)], [("ps", bB)])
                    a = f % 2
                    if exp_only:
                        ACT(sig[a][:, :T], ps[:, bA, :T], AF.Exp, [("ps", bA)], [("sig", a)], scale=-1.0)
                        SIGM_FROM_E(sig[a][:, :T], ("sig", a))
                        TT("dve", sig[a][:, :T], ps[:, bA, :T], sig[a][:, :T], ALU.mult, [("ps", bA), ("sig", a)], [("sig", a)])
                    else:
                        ACT(sig[a][:, :T], ps[:, bA, :T], AF.Silu, [("ps", bA)], [("sig", a)])
                    TT("dve", hmid[:, f, :T], sig[a][:, :T], ps[:, bB, :T], ALU.mult, [("sig", a), ("ps", bB)], [("hmid", f)])
            dv = wdn_bf[l].rearrange("(f p) n -> p f n", p=128)
            FG = [(0, 8), (8, 16), (16, 22)]
            for ch in range(2):
                sls = []
                for (f0, f1) in FG:
                    sls.append(load_slab([(0, 512, f1 - f0, dv[:, f0:f1, ch * 512:(ch + 1) * 512])], WK[f"dn{l}"]))
                for j in range(4):
                    mo = 4 * ch + j
                    b = banks[mo % 2]
                    for f in range(NF):
                        gi = f // 8
                        sl, sid = sls[gi]
                        MM(ps[:, b, :T], sl[:, f - 8 * gi, j * 128:(j + 1) * 128], hmid[:, f, :T], f == 0, f == NF - 1,
                           SK1(sid) + [("hmid", f)], [("ps", b)])
                    TT("dve", hT[:, mo, :T], hT[:, mo, :T], ps[:, b, :T], ALU.add, [("hT", hb, mo), ("ps", b)], [("hT", hb, mo)])

        def proj_res(wbf2d, wk, hb, T, src, skeys, banks, bias_base=None):
            hT = hTs[hb]
            v = kc_view(wbf2d)
            for s in range(2):
                sl, sid = load_slab([(0, 512, 8, v[:, :, s * 512:(s + 1) * 512])], wk)
                for j in range(4):
                    mo = 4 * s + j
                    b = banks[mo % 2]
                    for kc in range(8):
                        MM(ps[:, b, :T], sl[:, kc, j * 128:(j + 1) * 128], src(kc), kc == 0, kc == 7,
                           SK1(sid) + skeys(kc), [("ps", b)])
                    if bias_base is not None:
                        STT(hT[:, mo, :T], ps[:, b, :T], vcol(bias_base + mo), hT[:, mo, :T], ALU.add, ALU.add,
                            [("ps", b), ("hT", hb, mo), "vecs"], [("hT", hb, mo)])
                    else:
                        TT("dve", hT[:, mo, :T], hT[:, mo, :T], ps[:, b, :T], ALU.add, [("hT", hb, mo), ("ps", b)], [("hT", hb, mo)])

        def mkblock(kind, si, blk, hb):
            prompt = kind == "p"
            T = TB if prompt else TS
            B = dict(kind=kind, prompt=prompt, si=si, blk=blk, hb=hb, T=T, TT_=min(128, T), NT=max(1, T // 128),
                     t0=blk * TB if prompt else 0)
            if prompt:
                B.update(xsrc=xp[si], ydst=yp[si], kdst=kp[si], vdst=vp[si], csdst=csp[si], s0=blk * TB)
            else:
                B.update(xsrc=xs, ydst=ys, kdst=ks, vdst=vs, csdst=css, s0=PAST)
            return B

        def front(B, hook=None):
            prompt, si, blk, hb, T, TT_, NT, t0 = (B[k] for k in ("prompt", "si", "blk", "hb", "T", "TT_", "NT", "t0"))
            hT = hTs[hb]
            xsrc, csdst = B["xsrc"], B["csdst"]
            if prompt:
                DMA("sp", f32a[:, 0:NT, :], xsrc[t0:t0 + T, :].rearrange("(nt p) d -> p nt d", p=128), (), allf, "in_x")
            else:
                DMA("sp", f32a[0:T, 0, :], xsrc[0:T, :], (), allf, "in_x")
                DMA("sp", f32a[0:30, 1, :], sconv[:, :], (), allf, "in_x")
            for i in range(8):
                b = 4 + i % 2
                for nt in range(NT):
                    TR(ps[:, b, nt * 128:nt * 128 + 128], f32a[:, nt, i * 128:(i + 1) * 128], ident[:, :],
                       allf + ["ident"], [("ps", b)])
                if i % 2 == 0:
                    ACT(hT[:, i, :T], ps[:, b, :T], AF.Copy, [("ps", b)], [("hT", hb, i)])
                else:
                    CP("dve", hT[:, i, :T], ps[:, b, :T], [("ps", b)], [("hT", hb, i)])
            if prompt and blk == 0:
                for i in range(8):
                    MSET("pool", ubf[:, i, 0:30], 0.0, [("uTp", i)])
            if prompt and blk > 0:
                for i in range(8):
                    CP("pool", ubf[:, i, 0:30], ubf[:, i, TB:TB + 30], [("uTm", i)], [("uTp", i)])
            if not prompt:
                for i in range(8):
                    b = 6 + i % 2
                    TR(ps[:, b, 0:128], f32a[:, 1, i * 128:(i + 1) * 128], ident[:, :], allf + ["ident"], [("ps", b)])
                    CP("dve", ubf[:, i, 0:30], ps[:, b, 0:30], [("ps", b)], [("uTp", i)])
            cs_out = (prompt and blk == NBLK - 1) or not prompt
            WT = min(128, T)
            rms_stats(hb, T, 7)
            d_, k_ = xn_dst(T)
            rms_apply(hb, V_AG, T, d_, k_)
            if hook is not None:
                hook()
            wv = kc_view(win_bf)
            for s in range(4):
                sl, sid = load_slab([(0, 256, 8, wv[:, :, s * 256:(s + 1) * 256]),
                                     (256, 512, 8, wv[:, :, D + s * 256:D + (s + 1) * 256])], WK["in"])
                for j in range(2):
                    i = 2 * s + j
                    bA, bB = (4, 5) if i % 2 == 0 else (6, 7)
                    for kc in range(8):
                        MM(ps[:, bA, :T], sl[:, kc, j * 128:(j + 1) * 128], xn[:, kc, :T], kc == 0, kc == 7,
                           [("slot", sid, 0), ("xn", kc)], [("ps", bA)])
                    for kc in range(8):
                        MM(ps[:, bB, :T], sl[:, kc, 256 + j * 128:256 + (j + 1) * 128], xn[:, kc, :T], kc == 0, kc == 7,
                           [("slot", sid, 1), ("xn", kc)], [("ps", bB)])
                    a = i % 2
                    ACT(sig[a][:, :T], ps[:, bB, :T], AF.Exp, [("ps", bB), "vecs"], [("sig", a)], bias=vcol(V_NBG + i), scale=-1.0)
                    SIGM_FROM_E(sig[a][:, :T], ("sig", a))
                    STT(ubf[:, i, 30:30 + T], ps[:, bA, :T], vcol(V_BIN + i), sig[a][:, :T], ALU.add, ALU.mult,
                        [("ps", bA), ("sig", a), "vecs"], [("uTm", i)])
                    if cs_out:
                        STT(utail[:, i, 0:WT], ps[:, bA, T - WT:T], vcol(V_BIN + i), sig[a][:, T - WT:T], ALU.add, ALU.mult,
                            [("ps", bA), ("sig", a), "vecs"], [("utail", i)])
            if cs_out:
                for i in range(8):
                    b = 4 + i // 4
                    TR(ps[:, b, (i % 4) * 128:(i % 4) * 128 + 128], utail[:, i, 0:128], ident[:, :],
                       [("utail", i), "ident"], [("ps", b)])
                for h in range(2):
                    CP("dve", f32a[:, 0, h * 512:(h + 1) * 512], ps[:, 4 + h, :], [("ps", 4 + h)], [("f32a", h)])
                DMA("sp", csdst[:, :], f32a[WT - 30:WT, 0, :], [("f32a", 0), ("f32a", 1)], (), "out_cs", final=True)
            for i in range(8):
                sl, sid = load_slab([(0, 512, 8, wdg_bf[i])], [("wdg", i)])
                b = 4 + i % 2
                for j in range(31):
                    e = j * 128
                    MM(ps[:, b, :T], sl[:, e // 512, e % 512:e % 512 + 128], ubf[:, i, j:j + T], j == 0, j == 30,
                       SK1(sid) + [("uTp", i), ("uTm", i)], [("ps", b)])
                TSC("dve", f8(i)[:, :T], ps[:, b, :T], vcol(V_BDW + i), None, ALU.add, ALU.bypass,
                    [("ps", b), "vecs"], [("f32a", i)])
            for i in range(8):
                CP("dve", sq[0][:, :T], f8(i)[:, :T], [("f32a", i)], [("sq", 0)])
                MM(ps[:, 6, :T], ones_bf, sq[0][:, :T], i == 0, i == 7, [("sq", 0), "cbf"], [("ps", 6)])
                ACT(sq[1][:, :T], f8(i)[:, :T], AF.Square, [("f32a", i)], [("sq", 1)])
                MM(ps[:, 7, :T], ones_bf, sq[1][:, :T], i == 0, i == 7, [("sq", 1), "cbf"], [("ps", 7)])
            TSC("dve", mean[:, :T], ps[:, 6, :T], vcol(V_INVD), None, ALU.mult, ALU.bypass, [("ps", 6), "vecs"], ["mean"])
            ACT(tmpf[:, :T], mean[:, :T], AF.Square, ["mean"], ["tmpf"])
            STT(rstd[:, :T], ps[:, 7, :T], vcol(V_INVD), tmpf[:, :T], ALU.mult, ALU.subtract, [("ps", 7), "tmpf", "vecs"], ["rstd"])
            ACT(tmpf[:, :T], rstd[:, :T], AF.Ln, ["rstd", "vecs"], ["tmpf"], bias=vcol(V_EPS), scale=1.0)
            ACT(rstd[:, :T], tmpf[:, :T], AF.Exp, ["tmpf"], ["rstd"], scale=-0.5)
            for i in range(8):
                a = i % 2
                TT("dve", sig[a][:, :T], f8(i)[:, :T], mean[:, :T], ALU.subtract, [("f32a", i), "mean"], [("sig", a)])
                TT("dve", sig[a][:, :T], sig[a][:, :T], rstd[:, :T], ALU.mult, [("sig", a), "rstd"], [("sig", a)])
                TSC("dve", sig[a][:, :T], sig[a][:, :T], vcol(V_LNG + i), vcol(V_LNB + i), ALU.mult, ALU.add,
                    [("sig", a), "vecs"], [("sig", a)])
                et, ek = (tmpf, "tmpf") if a == 0 else (kvtok[0], ("kvtok", 0))
                ACT(et[:, :T], sig[a][:, :T], AF.Exp, [("sig", a)], [ek], scale=-1.0)
                SIGM_FROM_E(et[:, :T], ek)
                TT("dve", hmid[:, ST0 + i, :T], sig[a][:, :T], et[:, :T], ALU.mult, [("sig", a), ek], [("hmid", ST0 + i)])
            proj_res(wout_bf, WK["out"], hb, T, lambda kc: hmid[:, ST0 + kc, :T], lambda kc: [("hmid", ST0 + kc)],
                     (4, 5), bias_base=V_BOUT)
            ffn(0, hb, T, (4, 5, 6, 7), exp_only=True)

        def load_cache():
            for c in range(NPC):
                DMA("pool", hmid[:, c // 2, (c % 2) * 256:(c % 2) * 256 + 256], ck[c * 128:(c + 1) * 128, :], (),
                    [("hmid", c // 2)], ("in_ck", c % 2))
            DMA("pool", Vt[:, 0:NPC, :], cv.rearrange("(c p) n -> p c n", p=128), (), [("Vt", c) for c in range(NPC)], "in_cv")
            for c in range(NPC):
                b = c % 2
                for g in range(4):
                    MM(ps[0:64, b, g * 128:(g + 1) * 128], hmid[:, c // 2, (c % 2) * 256 + g * 64:(c % 2) * 256 + (g + 1) * 64],
                       cbf[:, C_ID:C_ID + 128], g == 0, g == 3, [("hmid", c // 2), "cbf"], [("ps", b)])
                for g in range(4):
                    if g % 2 == 0:
                        ACT(KT[0:64, g, c * 128:(c + 1) * 128], ps[0:64, b, g * 128:(g + 1) * 128], AF.Copy, [("ps", b)], [("KT", c // 4, g)])
                    else:
                        CP("dve", KT[0:64, g, c * 128:(c + 1) * 128], ps[0:64, b, g * 128:(g + 1) * 128], [("ps", b)], [("KT", c // 4, g)])

        def qidx(hd):
            r = hd % 4
            return 4 * (hd // 4) + (r % 2) * 2 + r // 2

        def kvq(B):
            prompt, si, blk, hb, T, TT_, NT, t0, s0 = (B[k] for k in ("prompt", "si", "blk", "hb", "T", "TT_", "NT", "t0", "s0"))
            kdst, vdst = B["kdst"], B["vdst"]
            rms_stats(hb, T, 7)
            d_, k_ = xn_dst(T)
            rms_apply(hb, V_KVG, T, d_, k_)
            sl, sid = load_slab([(0, 512, 8, kc_view(wkv_bf)[:, :, :])], WK["kv"])
            for nt in range(NT):
                b = nt % 2
                a = nt % 2
                for kc in range(8):
                    MM(ps[0:TT_, b, :], xn[:, kc, nt * 128:nt * 128 + TT_], sl[:, kc, :], kc == 0, kc == 7,
                       SK1(sid) + [("xn", kc)], [("ps", b)])
                ACT(kvtok[a][0:TT_, :], ps[0:TT_, b, :], AF.Copy, [("ps", b)], [("kvtok", a)])
                vc = (s0 + nt * 128) // 128
                CP("dve", Vt[0:TT_, vc, :], ps[0:TT_, b, 256:512], [("ps", b)], [("Vt", vc)])
                r0 = t0 + nt * 128 if prompt else 0
                DMA("sp", kdst[r0:r0 + TT_, :], kvtok[a][0:TT_, 0:256], [("kvtok", a)], (), ("out_kv", a), final=True)
                DMA("sp", vdst[r0:r0 + TT_, :], kvtok[a][0:TT_, 256:512], [("kvtok", a)], (), ("out_kv", a), final=True)
            kblk = ("KT", blk if prompt else "s")
            for g in range(4):
                b = 2 + g % 2
                for kc in range(8):
                    MM(ps[0:64, b, :T], sl[:, kc, g * 64:(g + 1) * 64], xn[:, kc, :T], kc == 0, kc == 7,
                       SK1(sid) + [("xn", kc)], [("ps", b)])
                ACT(KT[0:64, g, s0:s0 + T], ps[0:64, b, :T], AF.Copy, [("ps", b)], [kblk + (g,)])
            d_, k_ = xn_dst(T)
            rms_apply(hb, V_BG, T, d_, k_)
            qv = kc_view(wq_bf)
            for s in range(2):
                sl, sid = load_slab([(0, 512, 8, qv[:, :, s * 512:(s + 1) * 512])], WK["q"])
                for hh in range(8):
                    hd = 8 * s + hh
                    b = 4 + hd % 2
                    for kc in range(8):
                        MM(ps[0:64, b, :T], sl[:, kc, hh * 64:(hh + 1) * 64], xn[:, kc, :T], kc == 0, kc == 7,
                           SK1(sid) + [("xn", kc)], [("ps", b)])
                    qi = qidx(hd)
                    ACT(QT[0:64, qi, :T], ps[0:64, b, :T], AF.Copy, [("ps", b)], [("QT", qi)], scale=0.125)

        def attn(B):
            prompt, si, blk, T, TT_, NT, t0 = (B[k] for k in ("prompt", "si", "blk", "T", "TT_", "NT", "t0"))
            TQ = TT_
            W4 = 4 * TQ
            units = []
            for g in range(4):
                for qbl in range(NT):
                    if prompt:
                        last = (t0 // 128) + qbl
                        cl = [(c * 128, 128, c == last, ("KT", c // 4, g), c) for c in range(last, -1, -1)]
                    else:
                        cl = [(PAST, TS, True, ("KT", "s", g), NPC)]
                        cl += [(c * 128, 128, False, ("KT", c // 4, g), c) for c in range(NPC - 1, -1, -1)]
                    for ci, (koff, kn, diag, kkey, vc) in enumerate(cl):
                        units.append(dict(g=g, qbl=qbl, koff=koff, kn=kn, diag=diag, kkey=kkey, vc=vc,
                                          first=ci == 0, lastc=ci == len(cl) - 1, carry=None))
            NU = len(units)
            mk = mask_bf if TQ == 128 else mask64_bf
            scur = [0]

            def stageZ(k):
                u = units[k]; p = k % 3; kn = u["kn"]; g = u["g"]
                q0 = u["qbl"] * 128
                MM(ps[0:kn, p, 0:W4], KT[0:64, g, u["koff"]:u["koff"] + kn], QT[0:64, 4 * g:4 * g + 4, q0:q0 + TQ],
                   True, True, [u["kkey"]] + [("QT", 4 * g + x) for x in range(4)], [("ps", p)])
                ACT(Ets[k % 2][0:kn, :W4], ps[0:kn, p, :W4], AF.Exp, [("ps", p)], [("Et", k % 2)])

            def stageL(k):
                u = units[k]; l = k % 3; kn = u["kn"]
                ACT(Lt[l][0:kn, :W4], Ets[k % 2][0:kn, :W4], AF.Ln, [("Et", k % 2), "vecs"], [("Lt", l)],
                    bias=vecs[0:kn, V_ONE:V_ONE + 1], scale=1.0)
                if u["diag"]:
                    TT("pool", Lt[l][0:kn, :W4], Lt[l][0:kn, :W4], mk[0:kn, :W4], ALU.mult, [("Lt", l), "cbf"], [("Lt", l)])
                if u["first"]:
                    u["carry"] = None
                    if not u["lastc"]:
                        scur[0] = 0
                        if kn < 128:
                            MSET("pool", Sb[0][:, :W4], 0.0, [("Sb", 0)])
                        CP("pool", Sb[0][0:kn, :W4], Lt[l][0:kn, :W4], [("Lt", l)], [("Sb", 0)])
                else:
                    u["carry"] = scur[0]
                    if not u["lastc"]:
                        o = scur[0]; n = 1 - o
                        TT("pool", Sb[n][:, :W4], Sb[o][:, :W4], Lt[l][:, :W4], ALU.add, [("Sb", o), ("Lt", l)], [("Sb", n)])
                        scur[0] = n

            def stageT(k):
                u = units[k]; p = k % 3; l = k % 3; kn = u["kn"]
                MM(ps[0:kn, p, :W4], ntri_bf[0:kn, 0:kn], Lt[l][0:kn, :W4], False, u["first"], [("Lt", l), "cbf"], [("ps", p)])
                if not u["first"]:
                    c = u["carry"]
                    MM(ps[0:kn, p, :W4], nones_bf[:, 0:kn], Sb[c][:, :W4], False, True, [("Sb", c), "cbf"], [("ps", p)])
                a = k % 3
                ACT(At[a][0:kn, :W4], ps[0:kn, p, :W4], AF.Exp, [("ps", p)], [("At", a)])
                if u["diag"]:
                    TT("pool", At[a][0:kn, :W4], At[a][0:kn, :W4], mk[0:kn, :W4], ALU.mult, [("At", a), "cbf"], [("At", a)])

            def stageV(k):
                u = units[k]; a = k % 3; kn = u["kn"]; g = u["g"]
                ob = 3
                for rl in range(2):
                    MM(ps[rl * 64:(rl + 1) * 64, ob, 0:2 * TQ], Vt[0:kn, u["vc"], g * 64:(g + 1) * 64],
                       At[a][0:kn, rl * 2 * TQ:(rl + 1) * 2 * TQ], u["first"], u["lastc"],
                       [("Vt", u["vc"]), ("At", a)], [("ps", ob)])
                if u["lastc"]:
                    for rh in range(2):
                        oc = 2 * g + rh
                        dst = OT[:, oc, u["qbl"] * 128:u["qbl"] * 128 + TQ]
                        ACT(dst, ps[:, ob, rh * TQ:(rh + 1) * TQ], AF.Copy, [("ps", ob)], [("OT", oc)])

            for k in range(NU + 2):
                if k < NU:
                    stageZ(k)
                if 0 <= k - 1 < NU:
                    stageT(k - 1)
                if k < NU:
                    stageL(k)
                if 0 <= k - 2 < NU:
                    stageV(k - 2)

        def wo(B):
            hb, T = B["hb"], B["T"]
            proj_res(wo_bf, WK["o"], hb, T, lambda kc: OT[:, kc, :T], lambda kc: [("OT", kc)], (4, 5))

        def tail2(B):
            prompt, hb, T, TT_, NT, t0 = (B[k] for k in ("prompt", "hb", "T", "TT_", "NT", "t0"))
            ydst = B["ydst"]
            ffn(1, hb, T, (4, 5, 6, 7), exp_only=True)
            rms_stats(hb, T, 7)
            hT = hTs[hb]
            rms_apply(hb, V_FIN, T, lambda i: hT[:, i, 0:T], lambda i: [("hT", hb, i)])
            for nt in range(NT):
                for h in range(2):
                    b = 4 + h
                    for ii in range(4):
                        i = 4 * h + ii
                        TR(ps[:, b, ii * 128:(ii + 1) * 128], hT[:, i, nt * 128:nt * 128 + 128], ident[:, :],
                           [("hT", hb, i), "ident"], [("ps", b)])
                    if h == 0:
                        ACT(f32a[0:TT_, nt, 0:512], ps[0:TT_, b, :], AF.Copy, [("ps", b)], [("f32a", 2 * nt)])
                    else:
                        CP("dve", f32a[0:TT_, nt, 512:1024], ps[0:TT_, b, :], [("ps", b)], [("f32a", 2 * nt + 1)])
            if prompt:
                DMA("sp", ydst[t0:t0 + T, :].rearrange("(nt p) d -> p nt d", p=128), f32a[:, 0:NT, :], allf, (), "out_y", final=True)
            else:
                DMA("sp", ydst[0:T, :], f32a[0:T, 0, :], allf, (), "out_y", final=True)

        blocks = []
        for si in range(NP):
            for blk in range(NBLK):
                blocks.append(mkblock("p", si, blk, len(blocks) % 2))
        blocks.append(mkblock("s", 0, 0, len(blocks) % 2))
        NB = len(blocks)

        front(blocks[0], hook=build_diag)
        prep_rest()
        kvq(blocks[0])
        la, lb = [], []
        with S.thread(la):
            attn(blocks[0])
        if NB > 1:
            with S.thread(lb):
                front(blocks[1])
        S.merge(la, lb)
        for bi in range(NB):
            B = blocks[bi]
            wo(B)
            Bn = blocks[bi + 1] if bi + 1 < NB else None
            if Bn is not None:
                if not Bn["prompt"]:
                    load_cache()
                kvq(Bn)
            la, lb = [], []
            if Bn is not None:
                with S.thread(la):
                    attn(Bn)
            with S.thread(lb):
                tail2(B)
                if bi + 2 < NB:
                    front(blocks[bi + 2])
            S.merge(la, lb)
        cnt = S.emit()
    return nc, cnt


def pack_vecs(a_norm_g, conv_b_in, conv_w_dw, conv_b_dw, conv_ln_g, conv_ln_b, conv_b_out,
              kv_norm_g, b_norm_g, ffn_norm_g, final_norm_g):
    v = np.zeros((128, NV), np.float32)

    def put(base, vec):
        vec = np.asarray(vec, np.float32).reshape(-1, 128)
        v[:, base:base + vec.shape[0]] = vec.T

    put(V_AG, a_norm_g[0]); put(V_BIN, conv_b_in[0])
    wd = np.asarray(conv_w_dw[0], np.float32)
    for i in range(8):
        v[:, V_WDW + i * 31:V_WDW + (i + 1) * 31] = wd[:, i * 128:(i + 1) * 128].T
    put(V_BDW, conv_b_dw[0]); put(V_LNG, conv_ln_g[0]); put(V_LNB, conv_ln_b[0]); put(V_BOUT, conv_b_out[0])
    put(V_KVG, kv_norm_g); put(V_BG, b_norm_g[0]); put(V_FG, ffn_norm_g[0]); put(V_FG + 8, ffn_norm_g[1])
    put(V_FIN, final_norm_g)
    v[:, V_EPS] = EPS
    v[:, V_ONE] = 1.0
    v[:, V_INVD] = 1.0 / D
    v[:, V_NBG:V_NBG + 8] = -v[:, V_BIN + 8:V_BIN + 16]
    return v


def make_cst():
    c = np.zeros((128, NCST), np.float32)
    ar = np.arange(128)
    c[:, C_ID:C_ID + 128] = np.eye(128)
    c[:, C_ONES:C_ONES + 128] = 1.0
    c[:, C_NTRI:C_NTRI + 128] = -1.0 * (ar[:, None] >= ar[None, :])
    c[:, C_NONES:C_NONES + 128] = -1.0
    m = (ar[:, None] < ar[None, :]).astype(np.float32)
    c[:, C_MASK:C_MASK + 512] = np.tile(m, (1, 4))
    c[:, C_MASK64:C_MASK64 + 256] = np.tile(m[:, :64], (1, 4))
    return c


def run(inputs, n_cores, NP, runner=None):
    f = lambda k: np.ascontiguousarray(np.asarray(inputs[k], np.float32))
    x_prompt, x_sample = f("x_prompt"), f("x_sample")
    SEQ = x_prompt.shape[1]; TS = x_sample.shape[1]; PAST = inputs["cache_k"].shape[1]
    nc, cnt = build(NP, SEQ, PAST, TS)
    vecs = pack_vecs(*[f(k) for k in ("a_norm_g", "conv_b_in", "conv_w_dw", "conv_b_dw", "conv_ln_g", "conv_ln_b",
                                      "conv_b_out", "kv_norm_g", "b_norm_g", "ffn_norm_g", "final_norm_g")])
    cst = make_cst()
    shared = dict(w_in=f("conv_w_in")[0], w_out=f("conv_w_out")[0], w_kv=f("w_kv"), w_q=f("w_q")[0], w_o=f("w_o")[0],
                  w_gu=f("ffn_w_gu"), w_dn=f("ffn_w_down"), vecs=vecs, cst=cst)
    sc, ckf, cvf = f("state_conv"), f("cache_k"), f("cache_v")
    in_maps = []
    for c in range(n_cores):
        m = dict(shared)
        m["xp"] = np.ascontiguousarray(x_prompt[c * NP:(c + 1) * NP])
        m["xs"] = np.ascontiguousarray(x_sample[c])
        m["sconv"] = np.ascontiguousarray(sc[0, c])
        m["ck"] = np.ascontiguousarray(ckf[c].reshape(PAST, 256))
        m["cv"] = np.ascontiguousarray(cvf[c].reshape(PAST, 256))
        in_maps.append(m)
    if runner is None:
        res = run_bass_kernel_spmd(nc, in_maps, core_ids=list(range(n_cores))).results
    else:
        res = runner(nc, in_maps)
    cat = lambda k: np.concatenate([np.asarray(r[k], np.float32) for r in res], axis=0)
    stk = lambda k: np.stack([np.asarray(r[k], np.float32) for r in res], axis=0)
    B = n_cores * NP
    y_prompt = cat("yp")
    y_sample = stk("ys")
    csp = cat("csp")[None]
    k_prompt = cat("kp").reshape(B, SEQ, 4, 64)
    v_prompt = cat("vp").reshape(B, SEQ, 4, 64)
    css = stk("css")[None]
    k_sample = stk("ks").reshape(n_cores, TS, 4, 64)
    v_sample = stk("vs").reshape(n_cores, TS, 4, 64)
    return (y_prompt, y_sample, csp, k_prompt, v_prompt, css, k_sample, v_sample)


def kernel(**inputs):
    return run(inputs, 8, 2)
```
